# Optimizing a Trainium2 kernel written in Bass

```python
import jax, jax.numpy as jnp
from jax import lax
import numpy as np

D_MODEL = 4096
BATCH = 4
SEQ = 4096
DEPTH = 1

H_A = 16
DH_A = 128
H_B = 32
HKV_B = 4
G_B = H_B // HKV_B
DH_B = 64
WINDOW = 128
NUM_BUCKETS = 32
MAX_DISTANCE = 128
BLOCK = 128
D_FF = ((8 * D_MODEL // 3 + 255) // 256) * 256
EPS = 1e-6

W_QA = H_A * DH_A
W_KA = H_A * DH_A
W_VA = H_A * DH_A
W_FA = H_A
W_QB = H_B * DH_B
W_KB = HKV_B * DH_B
W_VB = HKV_B * DH_B
W_GA = D_MODEL
W_GB = D_MODEL
W_IN = W_QA + W_KA + W_VA + W_FA + W_QB + W_KB + W_VB + W_GA + W_GB

kernel_name = "fox_swa_sink_gated_hybrid_block"


def rms_norm(x, g):
    xf = x.astype(jnp.float32)
    y = xf * lax.rsqrt(jnp.mean(xf * xf, axis=-1, keepdims=True) + EPS)
    return (y * g.astype(jnp.float32)).astype(x.dtype)


def t5_bucket(dist):
    max_exact = NUM_BUCKETS // 2
    small = dist < max_exact
    large = max_exact + (np.log(np.maximum(dist, 1) / max_exact) / np.log(MAX_DISTANCE / max_exact)
                         * (NUM_BUCKETS - max_exact)).astype(np.int64)
    large = np.minimum(large, NUM_BUCKETS - 1)
    return np.where(small, dist, large)


def band_geometry(n_blocks):
    ql = np.arange(BLOCK)[:, None]
    kl = np.arange(2 * BLOCK)[None, :]
    dist = ql + BLOCK - kl
    in_window = (dist >= 0) & (dist < WINDOW)
    key_global = np.arange(n_blocks)[:, None, None] * BLOCK - BLOCK + kl[None]
    mask = in_window[None] & (key_global >= 0)
    bucket = t5_bucket(np.clip(dist, 0, None))
    return jnp.asarray(mask), jnp.asarray(bucket.astype(np.int32))


def forgetting_attention(q, k, v, f_logit):
    B, S, H, D = q.shape
    n_blocks = S // BLOCK
    log_f = jax.nn.log_sigmoid(f_logit.astype(jnp.float32))
    c = lax.cumsum(log_f, axis=1).transpose(0, 2, 1)
    key_pos = jnp.arange(S)
    scale = D ** -0.5

    def one_block(i):
        start = i * BLOCK
        q_blk = lax.dynamic_slice_in_dim(q, start, BLOCK, axis=1)
        c_q = lax.dynamic_slice_in_dim(c, start, BLOCK, axis=2)
        s = jnp.einsum('bqhd,bkhd->bhqk', q_blk, k, preferred_element_type=jnp.float32) * scale
        s = s + c_q[..., None] - c[:, :, None, :]
        q_pos = start + jnp.arange(BLOCK)
        causal = key_pos[None, :] <= q_pos[:, None]
        p = jax.nn.softmax(jnp.where(causal, s, -jnp.inf), axis=-1)
        return jnp.einsum('bhqk,bkhd->bqhd', p.astype(v.dtype), v)

    out = lax.map(one_block, jnp.arange(n_blocks))
    return out.transpose(1, 0, 2, 3, 4).reshape(B, S, H * D)


def sliding_window_sink_attention(q, k, v, sinks, rel_bias):
    B, S = q.shape[:2]
    n_blocks = S // BLOCK
    mask, bucket = band_geometry(n_blocks)
    bias = rel_bias.astype(jnp.float32)[bucket].transpose(2, 0, 1).reshape(HKV_B, G_B, BLOCK, 2 * BLOCK)
    qb = q.reshape(B, n_blocks, BLOCK, HKV_B, G_B, DH_B)

    def band(t):
        tp = jnp.pad(t, ((0, 0), (BLOCK, 0), (0, 0), (0, 0))).reshape(B, n_blocks + 1, BLOCK, HKV_B, DH_B)
        return jnp.concatenate([tp[:, :-1], tp[:, 1:]], axis=2)

    kb, vb = band(k), band(v)
    s = jnp.einsum('bnqhgd,bnkhd->bnhgqk', qb, kb, preferred_element_type=jnp.float32) * (DH_B ** -0.5)
    s = jnp.where(mask[None, :, None, None], s + bias, -jnp.inf)
    sink = sinks.astype(jnp.float32).reshape(1, 1, HKV_B, G_B, 1, 1)
    m = jnp.maximum(jnp.max(s, axis=-1, keepdims=True), sink)
    p = jnp.exp(s - m)
    p = p / (jnp.sum(p, axis=-1, keepdims=True) + jnp.exp(sink - m))
    o = jnp.einsum('bnhgqk,bnkhd->bnqhgd', p.astype(v.dtype), vb)
    return o.reshape(B, S, H_B * DH_B)


def setup_inputs(seed: int = 0) -> dict:
    key = jax.random.key(seed)
    ks = jax.random.split(key, 16)
    f32 = jnp.float32

    def w(k, shape, fan_in):
        return jax.random.normal(k, shape, f32) * (fan_in ** -0.5)

    return {
        "x": jax.random.normal(ks[0], (BATCH, SEQ, D_MODEL), f32),
        "norm1_g": 1.0 + 0.02 * jax.random.normal(ks[1], (DEPTH, D_MODEL), f32),
        "w_in": w(ks[2], (DEPTH, D_MODEL, W_IN), D_MODEL),
        "b_forget": 0.1 * jax.random.normal(ks[3], (DEPTH, H_A), f32),
        "attn_sinks": 0.5 * jax.random.normal(ks[4], (DEPTH, H_B), f32),
        "rel_bias": 0.1 * jax.random.normal(ks[5], (NUM_BUCKETS, H_B), f32),
        "w_branch_a": w(ks[6], (DEPTH, H_A * DH_A, D_MODEL), H_A * DH_A),
        "w_branch_b": w(ks[7], (DEPTH, H_B * DH_B, D_MODEL), H_B * DH_B),
        "w_out": w(ks[8], (DEPTH, D_MODEL, D_MODEL), D_MODEL),
        "norm2_g": 1.0 + 0.02 * jax.random.normal(ks[9], (DEPTH, D_MODEL), f32),
        "w_ffn_gate": w(ks[10], (DEPTH, D_MODEL, D_FF), D_MODEL),
        "w_ffn_up": w(ks[11], (DEPTH, D_MODEL, D_FF), D_MODEL),
        "w_ffn_down": w(ks[12], (DEPTH, D_FF, D_MODEL), D_FF),
        "final_g": 1.0 + 0.02 * jax.random.normal(ks[13], (D_MODEL,), f32),
    }


def reference(x, norm1_g, w_in, b_forget, attn_sinks, rel_bias, w_branch_a, w_branch_b,
              w_out, norm2_g, w_ffn_gate, w_ffn_up, w_ffn_down, final_g):
    B, S, _ = x.shape
    split_at = list(np.cumsum([W_QA, W_KA, W_VA, W_FA, W_QB, W_KB, W_VB, W_GA])[:])
    for l in range(DEPTH):
        h = rms_norm(x, norm1_g[l])
        proj = jnp.einsum('bsd,dn->bsn', h, w_in[l])
        qa, ka, va, fa, qb, kb, vb, ga, gb = jnp.split(proj, split_at, axis=-1)
        qa = qa.reshape(B, S, H_A, DH_A)
        ka = ka.reshape(B, S, H_A, DH_A)
        va = va.reshape(B, S, H_A, DH_A)
        fa = fa + b_forget[l]
        qb = qb.reshape(B, S, HKV_B, G_B, DH_B)
        kb = kb.reshape(B, S, HKV_B, DH_B)
        vb = vb.reshape(B, S, HKV_B, DH_B)

        ya = jnp.einsum('bsc,cd->bsd', forgetting_attention(qa, ka, va, fa), w_branch_a[l])
        yb = jnp.einsum('bsc,cd->bsd', sliding_window_sink_attention(qb, kb, vb, attn_sinks[l], rel_bias),
                        w_branch_b[l])
        mixed = jax.nn.sigmoid(ga) * ya + jax.nn.sigmoid(gb) * yb
        x = x + jnp.einsum('bsd,de->bse', mixed, w_out[l])

        h = rms_norm(x, norm2_g[l])
        hidden = jax.nn.silu(jnp.einsum('bsd,df->bsf', h, w_ffn_gate[l])) * jnp.einsum('bsd,df->bsf', h, w_ffn_up[l])
        x = x + jnp.einsum('bsf,fd->bsd', hidden, w_ffn_down[l])
    return rms_norm(x, final_g)
```

```python
import numpy as np
from contextlib import ExitStack
import concourse.bass as bass
import concourse.mybir as mybir
from concourse.bass_utils import run_bass_kernel_spmd

F32 = mybir.dt.float32
BF16 = mybir.dt.bfloat16
AF = mybir.ActivationFunctionType
ALU = mybir.AluOpType

D = 4096
KC = 32
SEQ = 4096
H_A, DH_A = 16, 128
H_B, HKV_B, DH_B = 32, 4, 64
D_FF = 11008
W_IN = 16912
C_QA, C_KA, C_VA, C_FA, C_QB, C_KB, C_VB, C_GA, C_GB = 0, 2048, 4096, 6144, 6160, 8208, 8464, 8720, 12816
TOWN = 2048
TG = 1024
NEG = -30000.0
EPS = 1e-6
KSPLIT = [29, 29, 28]


class SemC:
    def __init__(self, h):
        self.h = h
        self.v = 0


class Buf:
    __slots__ = ("w", "r")

    def __init__(self):
        self.w = {}
        self.r = {}


def _merge(d, sc, v):
    cur = d.get(id(sc))
    if cur is None or cur[1] < v:
        d[id(sc)] = (sc, v)


class Eng:
    def __init__(self, name, sem, slots, skip_self=False):
        self.name, self.sem, self.slots, self.skip_self = name, sem, slots, skip_self
        self.si = 0
        self.ops = []
        self.waited = {}

    def wait(self, sc, v):
        if v <= 0 or self.waited.get(id(sc), 0) >= v:
            return
        self.waited[id(sc)] = v
        self.ops.append(("w", sc.h, v))


class Prog:
    def __init__(self, nc, es):
        self.nc = nc
        mk = lambda n: SemC(es.enter_context(nc.semaphore(n)))
        self.PE = Eng("pe", mk("c_pe"), [], skip_self=True)
        self.ACT = Eng("act", mk("c_act"), [mk(f"qa{i}") for i in range(4)])
        self.DVE = Eng("dve", mk("c_dve"), [])
        self.POOL = Eng("pool", mk("c_pool"), [mk(f"qg{i}") for i in range(6)])
        self.SP = Eng("sp", mk("c_sp"), [mk(f"qs{i}") for i in range(28)])
        self.flip = 0

    def _deps(self, eng, reads, writes, part):
        d = {}
        for b in reads:
            for sc, v in b.w.values():
                _merge(d, sc, v)
        for b in writes:
            for sc, v in b.w.values():
                _merge(d, sc, v)
            for sc, v in b.r.values():
                _merge(d, sc, v)
        for b in part:
            for sc, v in b.r.values():
                _merge(d, sc, v)
        for sc, v in d.values():
            if eng.skip_self and sc is eng.sem:
                continue
            eng.wait(sc, v)

    def _commit(self, sc, v, reads, writes, part):
        for b in reads:
            _merge(b.r, sc, v)
        for b in writes:
            b.w = {id(sc): (sc, v)}
            b.r = {}
        for b in part:
            _merge(b.w, sc, v)

    def op(self, eng, fn, reads=(), writes=(), part=()):
        self._deps(eng, reads, writes, part)
        eng.sem.v += 1
        eng.ops.append(("i", fn, eng.sem.h, 1))
        self._commit(eng.sem, eng.sem.v, reads, writes, part)

    def group(self, eng, fns, reads=(), writes=(), part=()):
        self._deps(eng, reads, writes, part)
        for f in fns[:-1]:
            eng.ops.append(("n", f))
        eng.sem.v += 1
        eng.ops.append(("i", fns[-1], eng.sem.h, 1))
        self._commit(eng.sem, eng.sem.v, reads, writes, part)

    def dma(self, q, out, in_, reads=(), writes=(), part=(), slow=False):
        sl = q.slots[q.si]
        q.si = (q.si + 1) % len(q.slots)
        q.wait(sl, sl.v)
        self._deps(q, reads, writes, part)
        sl.v += 16
        if slow:
            fn = lambda e, o=out, i=in_: e.dma_start(out=o, in_=i, allow_slow_non_contiguous=True)
        else:
            fn = lambda e, o=out, i=in_: e.dma_start(out=o, in_=i)
        q.ops.append(("i", fn, sl.h, 16))
        self._commit(sl, sl.v, reads, writes, part)

    def ev_engine(self):
        self.flip ^= 1
        return self.ACT if self.flip else self.DVE


def _emit(e, ops):
    for o in ops:
        if o[0] == "w":
            e.wait_ge(o[1], o[2])
        elif o[0] == "n":
            o[1](e)
        else:
            o[1](e).then_inc(o[2], o[3])


class Ring:
    def __init__(self, aps):
        self.items = [(ap, Buf()) for ap in aps]
        self.i = 0

    def next(self):
        it = self.items[self.i % len(self.items)]
        self.i += 1
        return it


def build(debug_out=(), stop=99):
    nc = bass.Bass("TRN2", target_bir_lowering=False)

    def din(name, shape):
        return nc.dram_tensor(name, list(shape), F32, kind="ExternalInput").ap()

    _din = din

    def din(name, shape):
        if stop <= 0 and name.startswith("w_"):
            shape = [128, 128]
        return _din(name, shape)

    def dsc(name, shape, dt):
        kind = "ExternalOutput" if name in debug_out else "Internal"
        return nc.dram_tensor(name, list(shape), dt, kind=kind).ap()

    x_own = din("x_own", [TOWN, D])
    x_for = din("x_for", [TOWN, D])
    x_halo = din("x_halo", [512, D])
    n1g = din("n1g", [128, KC])
    n2g = din("n2g", [128, KC])
    fgv = din("fgv", [1, D])
    w_in = din("w_in", [D, W_IN])
    bfg = din("bfg", [1, 16])
    sinks = din("sinks", [32, 1])
    bias_raw = din("bias_raw", [128, 2 * 32 * 128])
    w_a = din("w_a", [2048, D])
    w_b = din("w_b", [2048, D])
    w_o = din("w_o", [D, D])
    w_g = din("w_g", [D, D_FF])
    w_u = din("w_u", [D, D_FF])
    w_d = din("w_d", [D_FF, D])
    cst = din("cst", [128, 640])
    e2c = din("e2c", [64, 4096])
    pcore = din("pcore", [128, 8])
    OUT = nc.dram_tensor("out", [TOWN, D], F32, kind="ExternalOutput").ap()

    QA = dsc("QA", [2048, TOWN], BF16)
    KA = dsc("KA", [2048, 4096], BF16)
    VA = dsc("VA", [4096, 2048], BF16)
    QB = dsc("QB", [2048, TOWN], BF16)
    KB = dsc("KB", [256, 2560], BF16)
    VB = dsc("VB", [2560, 256], BF16)
    GA = dsc("GA", [D, TOWN], BF16)
    GB = dsc("GB", [D, TOWN], BF16)
    OA = dsc("OA", [2048, TOWN], BF16)
    OB = dsc("OB", [2048, TOWN], BF16)
    MX = dsc("MX", [D, TOWN], BF16)
    X1 = dsc("X1", [TOWN, D], F32)
    HD = dsc("HD", [D_FF, TOWN], BF16)
    Y2 = dsc("Y2", [TOWN, D], F32)
    CDBG = dsc("CDBG", [128, 512], F32)
    dram_bufs = {}

    def db(name):
        if name not in dram_bufs:
            dram_bufs[name] = Buf()
        return dram_bufs[name]

    es = ExitStack()
    with es:
        P = Prog(nc, es)
        PE, ACT, DVE, POOL, SP = P.PE, P.ACT, P.DVE, P.POOL, P.SP
        sb = lambda n, shape, dt: es.enter_context(nc.sbuf_tensor(n, list(shape), dt))
        BIGA = sb("BIGA", [128, 32768], BF16)
        WT = [sb(f"WT{i}", [128, 16384], BF16) for i in range(2)]
        XB = sb("XB", [128, 4096], F32)
        XN = [sb(f"XN{i}", [128, 4096], BF16) for i in range(2)]
        f32pool = [sb(f"f32p{i}", [128, 512], F32) for i in range(8)]
        b16pool = [sb(f"b16p{i}", [128, 512], BF16) for i in range(8)]
        CST = sb("CST", [128, 640], F32)
        IDB = sb("IDB", [128, 128], BF16)
        TRIB = sb("TRIB", [128, 128], BF16)
        ONESF = sb("ONESF", [128, 128], F32)
        ONESB = sb("ONESB", [128, 128], BF16)
        PCORE = sb("PCORE", [128, 8], F32)
        GT1 = sb("GT1", [128, KC], F32)
        GT2 = sb("GT2", [128, KC], F32)
        BFB = sb("BFB", [128, 16], F32)
        LF = sb("LF", [128, 512], F32)
        CC = sb("CC", [128, 512], F32)
        NEGC = sb("NEGC", [128, 512], F32)
        TOT = sb("TOT", [128, 512], F32)
        OFFB = sb("OFFB", [128, 512], F32)
        SBT = sb("SBT", [128, 8 * 16], F32)
        OFS = sb("OFS", [128, 8 * 16], F32)
        PFX = sb("PFX", [128, 4 * 16], F32)
        SMALL = sb("SMALL", [128, 64], F32)
        FTMP = sb("FTMP", [128, 6 * 16], F32)
        ESK = sb("ESK", [64, 8], F32)
        ESKB = sb("ESKB", [64, 8], BF16)
        ESL2 = sb("ESL2", [128, 128], BF16)
        PSB = [es.enter_context(nc.psum_tensor(f"ps{i}", [128, 512], F32)) for i in range(8)]
        block = es.enter_context(nc.Block())

        ident_f = CST[:, 0:128]
        utri_f = CST[:, 128:256]
        trimask = CST[:, 256:384]
        b_cst, b_idb, b_onesf, b_onesb, b_pcore = Buf(), Buf(), Buf(), Buf(), Buf()
        b_gt1, b_gt2, b_bfb = Buf(), Buf(), Buf()
        b_biga, b_wt = Buf(), [Buf(), Buf()]
        b_xb, b_xn = Buf(), [Buf(), Buf()]
        F32R = Ring([t[:, :] for t in f32pool])
        B16R = Ring([t[:, :] for t in b16pool])
        PSR = Ring([t[:, :] for t in PSB])
        smallr = Ring([SMALL[:, i:i + 1] for i in range(32)])

        P.dma(SP, CST[:, :], cst, writes=(b_cst,))
        P.dma(SP, PCORE[:, :], pcore, writes=(b_pcore,))
        P.dma(SP, GT1[:, :], n1g, writes=(b_gt1,))
        P.dma(SP, GT2[:, :], n2g, writes=(b_gt2,))
        P.dma(SP, BFB[:, :], bfg.partition_broadcast(128).rearrange("p a b -> p (a b)"), writes=(b_bfb,))
        P.dma(POOL, IDB[:, :], cst[:, 0:128], writes=(b_idb,))
        P.op(POOL, lambda e: e.memset(ONESF[:, :], 1.0), writes=(b_onesf,))
        P.op(POOL, lambda e: e.memset(ONESB[:, :], 1.0), writes=(b_onesb,))
        ALPHA = PCORE[:, 0:1]
        ALPHA1 = PCORE[:, 1:2]
        PEN = PCORE[:, 2:3]

        def build_hT(xsrc, row0, T, gt, b_gt, xsrc_bufs):
            hT = BIGA[:, 0:KC * T].rearrange("p (kc t) -> p kc t", kc=KC)
            import os
            nblk = int(os.environ.get('NBLK_DBG', T // 128))
            jmax = int(os.environ.get('JMAX_DBG', 4))
            noev = int(os.environ.get('NOEV_DBG', 0))
            xbs = [(XB[:, :], b_xb), (WT[1][:, 0:8192].bitcast(F32), b_wt[1])]
            sm = [(smallr.next(), smallr.next()) for _ in range(nblk)]
            for (ss, bss), _ in sm:
                P.op(DVE, lambda e, ss=ss: e.memset(ss, 0.0), writes=(bss,))
            for tb in range(nblk):
                xb, bxb = xbs[tb % 2]
                xn, bxn = XN[tb % 2][:, :], b_xn[tb % 2]
                P.dma(SP, xb, xsrc[row0 + tb * 128: row0 + (tb + 1) * 128, :], reads=xsrc_bufs, writes=(bxb,))
                (ss, bss), (rs, brs) = sm[tb]
                P.op(ACT, lambda e, xn=xn, xb=xb, ss=ss: e.activation(out=xn, in_=xb, func=AF.Square, accum_out=ss),
                     reads=(bxb,), writes=(bxn, bss))
                P.op(DVE, lambda e, rs=rs, ss=ss: e.tensor_scalar(out=rs, in0=ss, scalar1=1.0 / D, scalar2=EPS,
                                                                  op0=ALU.mult, op1=ALU.add), reads=(bss,), writes=(brs,))
                P.op(ACT, lambda e, rs=rs: e.activation(out=rs, in_=rs, func=AF.Sqrt), reads=(brs,), writes=(brs,))
                P.op(DVE, lambda e, rs=rs: e.reciprocal(rs, rs), reads=(brs,), writes=(brs,))
                P.op(ACT, lambda e, xn=xn, xb=xb, rs=rs: e.activation(out=xn[:, 0:1536], in_=xb[:, 0:1536], func=AF.Copy, scale=rs),
                     reads=(bxb, brs), writes=(bxn,))
                P.op(POOL, lambda e, xn=xn, xb=xb, rs=rs: e.tensor_scalar(out=xn[:, 1536:4096], in0=xb[:, 1536:4096], scalar1=rs, scalar2=0.0,
                                                                          op0=ALU.mult, op1=ALU.add),
                     reads=(bxb, brs), part=(bxn,))
                for j in range(4):
                    ps, bps = PSR.next()
                    pv = ps.bitcast(BF16).rearrange("p (c t) -> p c t", c=8)
                    fns = [lambda e, o=pv[:, c, :], i=xn[:, (j * 8 + c) * 128:(j * 8 + c + 1) * 128]:
                           e.transpose(out=o, in_=i, identity=IDB[:, :]) for c in range(8)]
                    P.group(PE, fns, reads=(bxn, b_idb), writes=(bps,))
                    dst = hT[:, j * 8:(j + 1) * 8, tb * 128:(tb + 1) * 128]
                    gsc = gt[:, j * 8:(j + 1) * 8].unsqueeze(2).to_broadcast([128, 8, 128])
                    P.op(DVE, lambda e, o=dst, i=pv, g_=gsc: e.tensor_tensor(o, i, g_, ALU.mult), reads=(bps, b_gt), part=(b_biga,))
            return hT

        wt_i = [0]

        def next_wt(kcn, width):
            i = wt_i[0] % 2
            wt_i[0] += 1
            v = WT[i][:, 0:kcn * width].rearrange("p (kc n) -> p kc n", kc=kcn)
            return v, b_wt[i]

        def wload(dst, bdst, src, k0, kcn, c0, ncols, first=True):
            srcv = src[k0:k0 + kcn * 128, c0:c0 + ncols].rearrange("(kc p) n -> p kc n", p=128)
            if first:
                P.dma(POOL, dst, srcv, writes=(bdst,))
            else:
                P.dma(POOL, dst, srcv, part=(bdst,))

        def store(dst, src, bsrc, dname):
            P.dma(SP, dst, src, reads=(bsrc,), part=(db(dname),))

        def evac_copy(ps, bps, scale, func=None):
            st, bst = B16R.next()
            st = st[:, 0:ps.shape[-1]]
            if func is not None:
                P.op(ACT, lambda e, o=st, i=ps, f=func: e.activation(out=o, in_=i, func=f), reads=(bps,), writes=(bst,))
                return st, bst
            eng = P.ev_engine()
            if eng is ACT:
                P.op(ACT, lambda e, o=st, i=ps, s=scale: e.activation(out=o, in_=i, func=AF.Copy, scale=float(s)),
                     reads=(bps,), writes=(bst,))
            else:
                P.op(DVE, lambda e, o=st, i=ps, s=scale: e.tensor_scalar_mul(o, i, float(s)), reads=(bps,), writes=(bst,))
            return st, bst

        pend = []

        def mm_flush():
            if not pend:
                return
            fns, reads, seen = [], [], set()
            for p in pend:
                fns += p[0]
                for b in p[1]:
                    if id(b) not in seen:
                        seen.add(id(b))
                        reads.append(b)
            P.group(PE, fns, reads=tuple(reads), writes=tuple(p[3] for p in pend))
            for p in list(pend):
                p[4](p[2], p[3])
            pend.clear()

        def mm_submit(fns, reads, ps, bps, evac):
            pend.append((fns, reads, ps, bps, evac))
            if len(pend) == 2:
                mm_flush()

        def fm_block(wt, bwt, col, kcn, A, bA, tb, evac):
            ps, bps = PSR.next()
            fns = [lambda e, o=ps, l=wt[:, kc, col:col + 128], r=A[:, kc, tb * 512:(tb + 1) * 512], kc=kc:
                   e.matmul(o, l, r, start=(kc == 0), stop=(kc == kcn - 1)) for kc in range(kcn)]
            mm_submit(fns, (bwt, bA), ps, bps, evac)

        def tm_block(wt, bwt, col, ncols, kcn, A, bA, tb128, evac):
            ps, bps = PSR.next()
            fns = [lambda e, o=ps[:, 0:ncols], l=A[:, kc, tb128 * 128:(tb128 + 1) * 128], r=wt[:, kc, col:col + ncols], kc=kc:
                   e.matmul(o, l, r, start=(kc == 0), stop=(kc == kcn - 1)) for kc in range(kcn)]
            mm_submit(fns, (bwt, bA), ps, bps, evac)

        def phase1_group(kind, xsrc, row0, T, tok0):
            hT = build_hT(xsrc, row0, T, GT1, b_gt1, ())
            ntb = T // 512
            ktok0 = tok0 if kind == "own" else 2048 + tok0

            def fm_tile(c0, ncols, dst, drow0, dtok0, dname, scale=1.0, func=None):
                wt, bwt = next_wt(KC, ncols)
                wload(wt, bwt, w_in, 0, KC, c0, ncols)
                for c in range(ncols // 128):
                    for tb in range(ntb):
                        def evac(ps, bps, c=c, tb=tb):
                            st, bst = evac_copy(ps, bps, scale, func)
                            store(dst[drow0 + c * 128: drow0 + (c + 1) * 128, dtok0 + tb * 512: dtok0 + (tb + 1) * 512],
                                  st, bst, dname)
                        fm_block(wt, bwt, c * 128, KC, hT, b_biga, tb, evac)
                mm_flush()

            def tm_tile(c0, ncols, dst, dtok0, dcol0, dname):
                wt, bwt = next_wt(KC, ncols)
                wload(wt, bwt, w_in, 0, KC, c0, ncols)
                for t8 in range(T // 128):
                    def evac(ps, bps, t8=t8):
                        st, bst = evac_copy(ps[:, 0:ncols], bps, 1.0)
                        st = st[:, 0:ncols]
                        store(dst[dtok0 + t8 * 128: dtok0 + (t8 + 1) * 128, dcol0:dcol0 + ncols], st, bst, dname)
                    tm_block(wt, bwt, 0, ncols, KC, hT, b_biga, t8, evac)
                mm_flush()

            if kind == "own":
                for i in range(4):
                    fm_tile(C_QA + i * 512, 512, QA, i * 512, tok0, "QA", scale=DH_A ** -0.5)
            if kind in ("own", "for"):
                for i in range(4):
                    fm_tile(C_KA + i * 512, 512, KA, i * 512, ktok0, "KA")
                for i in range(4):
                    tm_tile(C_VA + i * 512, 512, VA, ktok0, i * 512, "VA")
                wt, bwt = next_wt(KC, 16)
                wload(wt, bwt, w_in, 0, KC, C_FA, 16)
                for t8 in range(T // 128):
                    blk = ktok0 // 128 + t8
                    def evac(ps, bps, blk=blk):
                        ft = FTMP[:, :].rearrange("p (a b) -> p a b", a=6)
                        bft = db("FTMP")
                        P.op(DVE, lambda e: e.tensor_tensor(ft[:, 0, :], ps[:, 0:16], BFB[:, :], ALU.add),
                             reads=(bps, b_bfb), writes=(bft,))
                        P.op(ACT, lambda e: e.activation(out=ft[:, 1, :], in_=ft[:, 0, :], func=AF.Abs),
                             reads=(bft,), part=(bft,))
                        P.op(ACT, lambda e: e.activation(out=ft[:, 2, :], in_=ft[:, 1, :], func=AF.Exp, scale=-1.0),
                             reads=(bft,), part=(bft,))
                        P.op(ACT, lambda e: e.activation(out=ft[:, 3, :], in_=ft[:, 2, :], func=AF.Ln, bias=1.0),
                             reads=(bft,), part=(bft,))
                        P.op(DVE, lambda e: e.tensor_scalar_min(ft[:, 4, :], ft[:, 0, :], 0.0), reads=(bft,), part=(bft,))
                        P.op(DVE, lambda e, blk=blk: e.tensor_sub(LF[:, blk * 16:(blk + 1) * 16], ft[:, 4, :], ft[:, 3, :]),
                             reads=(bft,), part=(db("LF"),))
                    tm_block(wt, bwt, 0, 16, KC, hT, b_biga, t8, evac)
                    mm_flush()
            if kind == "own":
                for i in range(4):
                    fm_tile(C_QB + i * 512, 512, QB, i * 512, tok0, "QB", scale=DH_B ** -0.5)
            if kind in ("own", "halo"):
                btok0 = tok0 if kind == "own" else 2048 + tok0
                wt, bwt = next_wt(KC, 512)
                wload(wt, bwt, w_in, 0, KC, C_KB, 512)
                for c in range(2):
                    for tb in range(ntb):
                        def evac(ps, bps, c=c, tb=tb):
                            st, bst = evac_copy(ps, bps, 1.0)
                            store(KB[c * 128:(c + 1) * 128, btok0 + tb * 512: btok0 + (tb + 1) * 512], st, bst, "KB")
                        fm_block(wt, bwt, c * 128, KC, hT, b_biga, tb, evac)
                mm_flush()
                for t8 in range(T // 128):
                    def evac(ps, bps, t8=t8):
                        st, bst = evac_copy(ps[:, 0:256], bps, 1.0)
                        store(VB[btok0 + t8 * 128: btok0 + (t8 + 1) * 128, :], st[:, 0:256], bst, "VB")
                    tm_block(wt, bwt, 256, 256, KC, hT, b_biga, t8, evac)
                mm_flush()
            if kind == "own":
                for i in range(8):
                    fm_tile(C_GA + i * 512, 512, GA, i * 512, tok0, "GA", func=AF.Sigmoid)
                for i in range(8):
                    fm_tile(C_GB + i * 512, 512, GB, i * 512, tok0, "GB", func=AF.Sigmoid)

        if stop == 0:
            build_hT(x_own, 0, TG, GT1, b_gt1, ())
        if stop >= 1:
            phase1_group("own", x_own, 0, TG, 0)
        if stop >= 2:
            phase1_group("own", x_own, TG, TG, TG)
            phase1_group("for", x_for, 0, TG, 0)
            phase1_group("for", x_for, TG, TG, TG)
            phase1_group("halo", x_halo, 0, 512, 0)

        def retire(into, *bufs):
            for b in bufs:
                for sc, v in list(b.r.values()) + list(b.w.values()):
                    _merge(into.r, sc, v)

        def fresh(*srcs):
            nb = Buf()
            for b in srcs:
                for sc, v in list(b.r.values()) + list(b.w.values()):
                    _merge(nb.r, sc, v)
            return nb

        if stop >= 3:
            b_lf = db("LF")
            b_cc, b_negc, b_tot, b_offb, b_sbt, b_ofs, b_pfx = Buf(), Buf(), Buf(), Buf(), Buf(), Buf(), Buf()
            psc, bpsc = PSR.next()
            pst, bpst = PSR.next()
            P.group(PE, [lambda e: e.matmul(psc, utri_f, LF[:, :], start=True, stop=True)], reads=(b_cst, b_lf), writes=(bpsc,))
            P.group(PE, [lambda e: e.matmul(pst, ONESF[:, :], LF[:, :], start=True, stop=True)], reads=(b_onesf, b_lf), writes=(bpst,))
            P.op(DVE, lambda e: e.tensor_copy(TOT[:, :], pst), reads=(bpst,), writes=(b_tot,))
            tot4 = TOT[:, :].rearrange("p (s r h) -> p s r h", s=8, r=4)
            sbt3 = SBT[:, :].rearrange("p (s h) -> p s h", s=8)
            ofs3 = OFS[:, :].rearrange("p (s h) -> p s h", s=8)
            pfx3 = PFX[:, :].rearrange("p (s h) -> p s h", s=4)
            offb4 = OFFB[:, :].rearrange("p (s r h) -> p s r h", s=8, r=4)
            P.op(DVE, lambda e: e.tensor_add(sbt3, tot4[:, :, 0, :], tot4[:, :, 1, :]), reads=(b_tot,), writes=(b_sbt,))
            P.op(DVE, lambda e: e.tensor_add(sbt3, sbt3, tot4[:, :, 2, :]), reads=(b_tot, b_sbt), writes=(b_sbt,))
            P.op(DVE, lambda e: e.tensor_add(sbt3, sbt3, tot4[:, :, 3, :]), reads=(b_tot, b_sbt), writes=(b_sbt,))
            P.op(DVE, lambda e: e.memset(PFX[:, 0:16], 0.0), writes=(b_pfx,))
            for s in range(1, 4):
                P.op(DVE, lambda e, s=s: e.tensor_add(pfx3[:, s, :], pfx3[:, s - 1, :], sbt3[:, s - 1, :]),
                     reads=(b_sbt, b_pfx), writes=(b_pfx,))
                P.op(DVE, lambda e, s=s: e.tensor_add(pfx3[:, s, :], pfx3[:, s, :], sbt3[:, 4 + s - 1, :]),
                     reads=(b_sbt, b_pfx), writes=(b_pfx,))
            for s in range(4):
                P.op(DVE, lambda e, s=s: e.scalar_tensor_tensor(out=ofs3[:, s, :], in0=sbt3[:, 4 + s, :], scalar=ALPHA,
                                                                in1=pfx3[:, s, :], op0=ALU.mult, op1=ALU.add),
                     reads=(b_sbt, b_pfx, b_pcore), writes=(b_ofs,))
                P.op(DVE, lambda e, s=s: e.scalar_tensor_tensor(out=ofs3[:, 4 + s, :], in0=sbt3[:, s, :], scalar=ALPHA1,
                                                                in1=pfx3[:, s, :], op0=ALU.mult, op1=ALU.add),
                     reads=(b_sbt, b_pfx, b_pcore, b_ofs), writes=(b_ofs,))
            P.op(DVE, lambda e: e.tensor_copy(offb4[:, :, 0, :], ofs3), reads=(b_ofs,), writes=(b_offb,))
            for r in range(1, 4):
                P.op(DVE, lambda e, r=r: e.tensor_add(offb4[:, :, r, :], offb4[:, :, r - 1, :], tot4[:, :, r - 1, :]),
                     reads=(b_tot, b_offb), writes=(b_offb,))
            P.op(DVE, lambda e: e.tensor_tensor(CC[:, :], psc, OFFB[:, :], ALU.add), reads=(bpsc, b_offb), writes=(b_cc,))
            P.op(DVE, lambda e: e.tensor_scalar_mul(NEGC[:, :], CC[:, :], -1.0), reads=(b_cc,), writes=(b_negc,))
            if "CDBG" in debug_out:
                P.dma(SP, CDBG, CC[:, :], reads=(b_cc,), part=(db("CDBG"),))

        if stop >= 4:
            QTv = [BIGA[:, i * 2048:(i + 1) * 2048] for i in range(2)]
            KTv = [BIGA[:, 4096 + i * 4096: 4096 + (i + 1) * 4096] for i in range(2)]
            VHv = [BIGA[:, 12288 + i * 4096: 12288 + (i + 1) * 4096].rearrange("p (b d) -> p b d", b=32) for i in range(2)]
            W1 = WT[1]
            f32v = lambda lo, n: W1[:, lo:lo + 2 * n].bitcast(F32)
            DGv = [f32v(0, 128), f32v(256, 128)]
            A0, A1, A2 = f32v(1024, 512), f32v(2048, 512), f32v(3072, 512)
            HH, MM, LL = W1[:, 4096:4608], W1[:, 4608:5120], W1[:, 5120:5632]
            CQ3 = [W1[:, 6144:6656], W1[:, 6656:7168]]
            BK = [f32v(8192, 32), f32v(8256, 32)]
            BKP = [f32v(8320, 4), f32v(8328, 4)]
            RCOL = [f32v(8336, 1), f32v(8338, 1)]
            b_qt, b_kt, b_vh = [Buf(), Buf()], [Buf(), Buf()], [Buf(), Buf()]
            for b in b_qt + b_kt + b_vh:
                b.r = dict(b_biga.r)
                b.r.update(b_biga.w)
            mkw1 = lambda: fresh(b_wt[1])
            b_dg = [mkw1(), mkw1()]
            b_a0, b_a1, b_a2, b_hh, b_mm, b_ll = mkw1(), mkw1(), mkw1(), mkw1(), mkw1(), mkw1()
            b_cq3, b_bk, b_bkp, b_rcol = [mkw1(), mkw1()], [mkw1(), mkw1()], [mkw1(), mkw1()], [mkw1(), mkw1()]
            b_trib = Buf()
            P.dma(POOL, TRIB[:, :], cst[:, 256:384], writes=(b_trib,))
            for i in range(2):
                P.op(POOL, lambda e, i=i: e.memset(CQ3[i], 0.0), writes=(b_cq3[i],))
            VAv = VA.rearrange("(b p) c -> p b c", p=128)
            S_R = Ring([PSB[0][:, :], PSB[1][:, :], PSB[2][:, :]])
            OT_R = Ring([PSB[3][:, :], PSB[4][:, :]])
            RS_R = Ring([PSB[5][:, :], PSB[6][:, :]])
            psCQ, b_pscq = PSB[7][:, :], Buf()
            CC3 = CC[:, :].rearrange("p (b h) -> p b h", h=16)
            dgi = [0]

            def fox_head_loads(h):
                hb = h % 2
                P.dma(SP, QTv[hb], QA[h * 128:(h + 1) * 128, :], reads=(db("QA"),), writes=(b_qt[hb],))
                P.dma(SP, KTv[hb], KA[h * 128:(h + 1) * 128, :], reads=(db("KA"),), writes=(b_kt[hb],))
                P.dma(SP, VHv[hb], VAv[:, :, h * 128:(h + 1) * 128], reads=(db("VA"),), writes=(b_vh[hb],))

            def fox_prep(h, s):
                pb = (h * 4 + s) % 2
                for r in range(4):
                    blk = 4 * s + r
                    dg, bdg = DGv[dgi[0] % 2], b_dg[dgi[0] % 2]
                    dgi[0] += 1
                    P.op(DVE, lambda e, dg=dg, blk=blk: e.tensor_scalar_mul(dg, ident_f, CC[:, blk * 16 + h: blk * 16 + h + 1]),
                         reads=(b_cst, b_cc), writes=(bdg,))
                    P.group(PE, [lambda e, dg=dg, r=r: e.matmul(psCQ[:, r * 128:(r + 1) * 128], ONESF[:, :], dg, start=True, stop=True)],
                            reads=(b_onesf, bdg), writes=(b_pscq,) if r == 0 else (), part=() if r == 0 else (b_pscq,))
                rc_, bk_, bkp_, cq_ = RCOL[pb], BK[pb], BKP[pb], CQ3[pb]
                P.op(DVE, lambda e: e.tensor_copy(rc_, psCQ[:, 0:1]), reads=(b_pscq,), writes=(b_rcol[pb],))
                P.op(DVE, lambda e: e.tensor_scalar(out=A0[0:65, :], in0=psCQ[0:65, :], scalar1=rc_[0:65, :], scalar2=0.0, op0=ALU.subtract, op1=ALU.add),
                     reads=(b_pscq, b_rcol[pb]), writes=(b_a0,))
                P.op(DVE, lambda e: e.tensor_scalar(out=bk_, in0=CC3[:, :, h], scalar1=rc_, scalar2=-1.0, op0=ALU.subtract, op1=ALU.mult),
                     reads=(b_cc, b_rcol[pb]), writes=(b_bk[pb],))
                P.op(DVE, lambda e: e.tensor_scalar(out=bkp_, in0=bk_[:, 16 + 4 * s:20 + 4 * s], scalar1=PEN, scalar2=0.0, op0=ALU.add, op1=ALU.add),
                     reads=(b_bk[pb], b_pcore), writes=(b_bkp[pb],))
                P.op(DVE, lambda e: e.tensor_copy(HH[0:65, :], A0[0:65, :]), reads=(b_a0,), writes=(b_hh,))
                P.op(DVE, lambda e: e.tensor_tensor(A1[0:65, :], A0[0:65, :], HH[0:65, :], ALU.subtract), reads=(b_a0, b_hh), writes=(b_a1,))
                P.op(DVE, lambda e: e.tensor_copy(MM[0:65, :], A1[0:65, :]), reads=(b_a1,), writes=(b_mm,))
                P.op(DVE, lambda e: e.tensor_tensor(A2[0:65, :], A1[0:65, :], MM[0:65, :], ALU.subtract), reads=(b_a1, b_mm), writes=(b_a2,))
                P.op(DVE, lambda e: e.tensor_copy(cq_[0:1, :], HH[0:1, :]), reads=(b_hh,), writes=(b_cq3[pb],))
                P.op(DVE, lambda e: e.tensor_copy(cq_[32:33, :], MM[32:33, :]), reads=(b_mm,), part=(b_cq3[pb],))
                P.op(DVE, lambda e: e.tensor_copy(cq_[64:65, :], A2[64:65, :]), reads=(b_a2,), part=(b_cq3[pb],))

            slots = [(h, s) for h in range(H_A) for s in range(4)]
            blocks = []
            for (h, s) in slots:
                lst = [("for", j, r) for j in range(s + 1) for r in range(4)] + \
                      [("own", j, r) for j in range(s + 1) for r in range(4)]
                for n, (kind, j, r) in enumerate(lst):
                    blocks.append(dict(h=h, s=s, kind=kind, j=j, r=r, first=(n == 0), last=(n == len(lst) - 1)))
            state = {}

            def fox_qk(bk):
                h, s, kind, j, r = bk["h"], bk["s"], bk["kind"], bk["j"], bk["r"]
                hb, pb = h % 2, (h * 4 + s) % 2
                kblk = (0 if kind == "own" else 16) + 4 * j + r
                diag = (kind == "own" and j == s)
                qlo = r * 128 if diag else 0
                ps, bps = S_R.next()
                fns = [lambda e: e.matmul(ps[:, qlo:512], KTv[hb][:, kblk * 128:(kblk + 1) * 128],
                                          QTv[hb][:, s * 512 + qlo:(s + 1) * 512], start=True, stop=False),
                       lambda e: e.matmul(ps[:, qlo:512], ONESB[0:65, :], CQ3[pb][0:65, qlo:512], start=False, stop=not diag)]
                if diag:
                    fns.append(lambda e: e.matmul(ps[:, qlo:qlo + 128], IDB[:, :], TRIB[:, :], start=False, stop=True))
                P.group(PE, fns, reads=(b_kt[hb], b_qt[hb], b_onesb, b_cq3[pb], b_idb, b_trib), writes=(bps,))
                bk.update(ps=ps, bps=bps, kblk=kblk, diag=diag, qlo=qlo)

            def fox_rest(bk):
                h, s, kind, j, r = bk["h"], bk["s"], bk["kind"], bk["j"], bk["r"]
                hb, pb = h % 2, (h * 4 + s) % 2
                ps, bps, kblk, qlo = bk["ps"], bk["bps"], bk["kblk"], bk["qlo"]
                pt, bpt = B16R.next()
                if kind == "for" and j == s:
                    bias, bbias = BKP[pb][:, r:r + 1], b_bkp[pb]
                else:
                    bias, bbias = BK[pb][:, kblk:kblk + 1], b_bk[pb]
                P.op(ACT, lambda e: e.activation(out=pt[:, qlo:512], in_=ps[:, qlo:512], func=AF.Exp, bias=bias),
                     reads=(bps, bbias), writes=(bpt,))
                if bk["first"]:
                    state["ot"], state["bot"] = OT_R.next()
                    state["rs"], state["brs"] = RS_R.next()
                ot, bot, rs, brs = state["ot"], state["bot"], state["rs"], state["brs"]
                fst, lst_ = bk["first"], bk["last"]
                P.group(PE, [lambda e: e.matmul(ot[:, qlo:512], VHv[hb][:, kblk, :], pt[:, qlo:512], start=fst, stop=lst_),
                             lambda e: e.matmul(rs[:, qlo:512], ONESB[:, :], pt[:, qlo:512], start=fst, stop=lst_)],
                        reads=(b_vh[hb], bpt, b_onesb), writes=(bot, brs) if fst else (), part=() if fst else (bot, brs))
                if lst_:
                    ln_, bln = F32R.next()
                    rc, brc = F32R.next()
                    ob, bob = B16R.next()
                    P.op(ACT, lambda e: e.activation(out=ln_, in_=rs, func=AF.Ln), reads=(brs,), writes=(bln,))
                    P.op(ACT, lambda e: e.activation(out=rc, in_=ln_, func=AF.Exp, scale=-1.0), reads=(bln,), writes=(brc,))
                    P.op(DVE, lambda e: e.tensor_tensor(ob, ot, rc, ALU.mult), reads=(bot, brc), writes=(bob,))
                    store(OA[h * 128:(h + 1) * 128, s * 512:(s + 1) * 512], ob, bob, "OA")

            LOOK = 2
            fox_head_loads(0)
            fox_prep(0, 0)
            nblk_ = len(blocks)
            for n in range(min(LOOK, nblk_)):
                fox_qk(blocks[n])
            for n, bk in enumerate(blocks):
                if bk["first"]:
                    si = slots.index((bk["h"], bk["s"]))
                    if si + 1 < len(slots):
                        nh, ns = slots[si + 1]
                        if ns == 0:
                            fox_head_loads(nh)
                        fox_prep(nh, ns)
                if n + LOOK < nblk_:
                    fox_qk(blocks[n + LOOK])
                fox_rest(bk)
            retire(b_biga, *(b_qt + b_kt + b_vh))
            retire(b_wt[1], *(b_dg + [b_a0, b_a1, b_a2, b_hh, b_mm, b_ll] + b_cq3 + b_bk + b_bkp + b_rcol))


        if stop >= 5:
            KBs = BIGA[:, 0:10240].rearrange("p (g t) -> p g t", g=4)
            VBs = BIGA[:, 10240:15360].rearrange("p (b c) -> p b c", b=20)
            QBv = [BIGA[:, 20480 + i * 4096: 20480 + (i + 1) * 4096].rearrange("p (h q) -> p h q", h=32) for i in range(2)]
            BIASM = WT[0][:, 0:16384].bitcast(F32).rearrange("p (c h q) -> p c h q", c=2, h=32)
            E2 = WT[1][:, 0:4096]
            b_kbs, b_vbs, b_qbv, b_biasm, b_e2 = fresh(b_biga), fresh(b_biga), [fresh(b_biga), fresh(b_biga)], fresh(b_wt[0]), fresh(b_wt[1])
            b_esk, b_esl = Buf(), Buf()
            P.dma(SP, KBs[0:64, :, :], KB.rearrange("(g d) t -> d g t", d=64), reads=(db("KB"),), writes=(b_kbs,))
            P.dma(SP, VBs, VB.rearrange("(b p) c -> p b c", p=128), reads=(db("VB"),), writes=(b_vbs,))
            for g_ in range(4):
                for c_ in range(2):
                    P.op(DVE, lambda e, g_=g_, c_=c_: e.memset(KBs[64:128, g_, c_ * 1280:(c_ + 1) * 1280], 0.0), part=(b_kbs,))
            for i_ in range(2):
                for c_ in range(2):
                    P.op(DVE, lambda e, i_=i_, c_=c_: e.memset(QBv[i_][64:128, c_ * 16:(c_ + 1) * 16, :], 0.0),
                         writes=(b_qbv[i_],) if c_ == 0 else (), part=() if c_ == 0 else (b_qbv[i_],))
            for c_ in range(2):
                P.op(DVE, lambda e, c_=c_: e.memset(E2[64:128, c_ * 2048:(c_ + 1) * 2048], 0.0), part=(b_e2,))
            P.op(POOL, lambda e: e.memset(ESL2[:, :], 0.0), writes=(b_esl,))
            P.dma(SP, WT[0][:, 0:16384].bitcast(F32), bias_raw, writes=(b_biasm,))
            P.dma(POOL, E2[0:64, :], e2c, writes=(b_e2,))
            for c in range(2):
                for h in range(32):
                    P.op(POOL, lambda e, c=c, h=h: e.tensor_tensor(BIASM[:, c, h, :], BIASM[:, c, h, :],
                                                                   CST[:, 384 + c * 128: 512 + c * 128], ALU.add),
                         reads=(b_cst,), writes=(b_biasm,))
            P.dma(SP, ESK[0:32, 0:1], sinks, writes=(b_esk,))
            P.dma(SP, ESK[32:64, 0:1], sinks, part=(b_esk,))
            P.op(ACT, lambda e: e.activation(out=ESK[:, 1:2], in_=ESK[:, 0:1], func=AF.Exp), reads=(b_esk,), part=(b_esk,))
            P.op(DVE, lambda e: e.tensor_copy(ESKB[:, 0:1], ESK[:, 1:2]), reads=(b_esk,), part=(b_esk,))
            P.op(DVE, lambda e: e.tensor_copy(ESK[:, 2:3], ESKB[:, 0:1]), reads=(b_esk,), part=(b_esk,))
            P.op(DVE, lambda e: e.tensor_sub(ESK[:, 3:4], ESK[:, 1:2], ESK[:, 2:3]), reads=(b_esk,), part=(b_esk,))
            P.op(DVE, lambda e: e.tensor_scalar_mul(ESL2[0:32, :], ONESF[0:32, :], ESK[0:32, 2:3]),
                 reads=(b_esk, b_onesf), part=(b_esl,))
            P.op(DVE, lambda e: e.tensor_scalar_mul(ESL2[32:64, :], ONESF[32:64, :], ESK[32:64, 3:4]),
                 reads=(b_esk, b_onesf), part=(b_esl,))
            QBd = QB.rearrange("(h d) t -> d h t", d=64)
            OBd = OB.rearrange("(h d) t -> d h t", d=64)
            import os
            KQ = 64 if os.environ.get('SWA_OLDQK') else 128
            OLDPV = bool(os.environ.get('SWA_OLDPV'))
            SS_R = Ring([PSB[0][:, :], PSB[1][:, :], PSB[2][:, :], PSB[3][:, :]])
            SOT_R = Ring([PSB[4][:, :], PSB[5][:, :]])
            SRS_R = Ring([PSB[6][:, :], PSB[7][:, :]])
            units = []
            for i in range(16):
                for g in range(4):
                    for hh in range(2):
                        units.append(dict(i=i, g=g, hh=hh))

            def swa_A(u):
                i, g, hh = u["i"], u["g"], u["hh"]
                s, r = i // 4, i % 4
                cur = i
                prev = i - 1 if r > 0 else 16 + s
                qb, bqb = QBv[i % 2], b_qbv[i % 2]
                if g == 0 and hh == 0:
                    P.dma(SP, qb[0:64, :, :], QBd[:, :, i * 128:(i + 1) * 128], reads=(db("QB"),), writes=(bqb,))
                h0 = g * 8 + hh * 4
                pts = []
                for which, kblk in ((1, cur), (0, prev)):
                    ps, bps = SS_R.next()
                    P.group(PE, [lambda e, ps=ps, kblk=kblk: e.matmul(ps, KBs[0:KQ, g, kblk * 128:(kblk + 1) * 128],
                                                                     qb[0:KQ, h0:h0 + 4, :], start=True, stop=True)],
                            reads=(b_kbs, bqb), writes=(bps,))
                    tmp, btmp = F32R.next()
                    pt, bpt = B16R.next()
                    P.op(DVE, lambda e, ps=ps, tmp=tmp, which=which: e.tensor_tensor(
                        tmp.rearrange("p (h q) -> p h q", h=4), ps.rearrange("p (h q) -> p h q", h=4),
                        BIASM[:, which, h0:h0 + 4, :], ALU.add), reads=(bps, b_biasm), writes=(btmp,))
                    if which == 0 and r == 0 and not os.environ.get('SWA_NOPEN'):
                        P.op(ACT, lambda e, pt=pt, tmp=tmp: e.activation(out=pt, in_=tmp, func=AF.Exp, bias=PCORE[:, 3 + s:4 + s]),
                             reads=(btmp, b_pcore), writes=(bpt,))
                    else:
                        P.op(ACT, lambda e, pt=pt, tmp=tmp: e.activation(out=pt, in_=tmp, func=AF.Exp),
                             reads=(btmp,), writes=(bpt,))
                    pts.append((pt, bpt, kblk))
                u.update(pts=pts, h0=h0)

            def swa_B(u):
                i, g, h0 = u["i"], u["g"], u["h0"]
                ot, bot = SOT_R.next()
                rs, brs = SRS_R.next()
                (ptc, bptc, kc_), (ptp, bptp, kp_) = u["pts"]
                vc0 = g * 64 if g < 3 else 128
                u["pr0"] = 0 if g < 3 else 64
                P.group(PE, [lambda e: e.matmul(ot, VBs[:, kc_, vc0:vc0 + 128], ptc, start=True, stop=False),
                             lambda e: e.matmul(ot, VBs[:, kp_, vc0:vc0 + 128], ptp, start=False, stop=True)],
                        reads=(b_vbs, bptc, bptp), writes=(bot,))
                P.group(PE, [lambda e: e.matmul(rs, ONESB[:, :], ptc, start=True, stop=False),
                             lambda e: e.matmul(rs, ONESB[:, :], ptp, start=False, stop=False),
                             lambda e: e.matmul(rs, ESL2[:, :], E2[:, h0 * 128:(h0 + 4) * 128], start=False, stop=True)],
                        reads=(b_onesb, bptc, bptp, b_esl, b_e2), writes=(brs,))
                u.update(ot=ot, bot=bot, rs=rs, brs=brs)

            def swa_C(u):
                i, h0 = u["i"], u["h0"]
                ot, bot, rs, brs = u["ot"], u["bot"], u["rs"], u["brs"]
                ln_, bln = F32R.next()
                rc, brc = F32R.next()
                ob, bob = B16R.next()
                pr = slice(u["pr0"], u["pr0"] + 64)
                P.op(ACT, lambda e: e.activation(out=ln_[pr, :], in_=rs[pr, :], func=AF.Ln), reads=(brs,), writes=(bln,))
                P.op(ACT, lambda e: e.activation(out=rc[pr, :], in_=ln_[pr, :], func=AF.Exp, scale=-1.0), reads=(bln,), writes=(brc,))
                P.op(DVE, lambda e: e.tensor_tensor(ob[pr, :], ot[pr, :], rc[pr, :], ALU.mult), reads=(bot, brc), writes=(bob,))
                store(OBd[:, h0:h0 + 4, i * 128:(i + 1) * 128], ob[pr, :].rearrange("p (h q) -> p h q", h=4), bob, "OB")

            swa_A(units[0])
            for n, u in enumerate(units):
                if n + 1 < len(units):
                    swa_A(units[n + 1])
                swa_B(u)
                swa_C(u)
            retire(b_biga, b_kbs, b_vbs, *b_qbv)
            retire(b_wt[0], b_biasm)
            retire(b_wt[1], b_e2)

        if stop >= 6:
            for tg in range(2):
                OAg = BIGA[:, 0:16384].rearrange("p (kc t) -> p kc t", kc=16)
                OBg = BIGA[:, 16384:32768].rearrange("p (kc t) -> p kc t", kc=16)
                b_oag, b_obg = fresh(b_biga), fresh(b_biga)
                P.dma(SP, OAg, OA.rearrange("(kc p) t -> p kc t", p=128)[:, :, tg * TG:(tg + 1) * TG], reads=(db("OA"),), writes=(b_oag,))
                P.dma(SP, OBg, OB.rearrange("(kc p) t -> p kc t", p=128)[:, :, tg * TG:(tg + 1) * TG], reads=(db("OB"),), writes=(b_obg,))
                for n in range(8):
                    wt, bwt = next_wt(32, 512)
                    wload(wt[:, 0:16, :], bwt, w_a, 0, 16, n * 512, 512)
                    wload(wt[:, 16:32, :], bwt, w_b, 0, 16, n * 512, 512, first=False)
                    def p4_unit(c, tb, n=n, tg=tg, wt=wt, bwt=bwt, OAg=OAg, OBg=OBg, b_oag=b_oag, b_obg=b_obg):
                        if True:
                            r0, t0 = n * 512 + c * 128, tg * TG + tb * 512
                            gat, bgat = B16R.next()
                            gbt, bgbt = B16R.next()
                            P.dma(SP, gat, GA[r0:r0 + 128, t0:t0 + 512], reads=(db("GA"),), writes=(bgat,))
                            P.dma(SP, gbt, GB[r0:r0 + 128, t0:t0 + 512], reads=(db("GB"),), writes=(bgbt,))
                            psa, bpsa = PSR.next()
                            psb, bpsb = PSR.next()
                            P.group(PE, [lambda e, kc=kc, psa=psa: e.matmul(psa, wt[:, kc, c * 128:(c + 1) * 128], OAg[:, kc, tb * 512:(tb + 1) * 512],
                                                                            start=(kc == 0), stop=(kc == 15)) for kc in range(16)] +
                                        [lambda e, kc=kc, psb=psb: e.matmul(psb, wt[:, 16 + kc, c * 128:(c + 1) * 128], OBg[:, kc, tb * 512:(tb + 1) * 512],
                                                                            start=(kc == 0), stop=(kc == 15)) for kc in range(16)],
                                    reads=(bwt, b_oag, b_obg), writes=(bpsa, bpsb))
                            t1, bt1 = F32R.next()
                            t2, bt2 = F32R.next()
                            mx, bmx = B16R.next()
                            P.op(DVE, lambda e, t1=t1, psa=psa, gat=gat: e.tensor_tensor(t1, psa, gat, ALU.mult), reads=(bpsa, bgat), writes=(bt1,))
                            P.op(DVE, lambda e, t2=t2, psb=psb, gbt=gbt: e.tensor_tensor(t2, psb, gbt, ALU.mult), reads=(bpsb, bgbt), writes=(bt2,))
                            P.op(POOL, lambda e, mx=mx, t1=t1, t2=t2: e.tensor_tensor(mx, t1, t2, ALU.add), reads=(bt1, bt2), writes=(bmx,))
                            store(MX[r0:r0 + 128, t0:t0 + 512], mx, bmx, "MX")
                    for c in range(4):
                        for tb in range(2):
                            p4_unit(c, tb)
                retire(b_biga, b_oag, b_obg)

        def resid_gemm(A, bA, kcn, wsrc, k0, res, res_name, dst, dst_name, tg, after_tile=None):
            for n in range(8):
                wt, bwt = next_wt(kcn, 512)
                wload(wt, bwt, wsrc, k0, kcn, n * 512, 512)
                for t8 in range(8):
                    rows = slice(tg * TG + t8 * 128, tg * TG + (t8 + 1) * 128)
                    cols = slice(n * 512, (n + 1) * 512)
                    xr, bxr = F32R.next()
                    P.dma(SP, xr, res[rows, cols], reads=(db(res_name),) if res_name else (), writes=(bxr,))

                    def evac(ps, bps, xr=xr, bxr=bxr, rows=rows, cols=cols):
                        st, bst = F32R.next()
                        P.op(DVE, lambda e: e.tensor_tensor(st, ps, xr, ALU.add), reads=(bps, bxr), writes=(bst,))
                        store(dst[rows, cols], st, bst, dst_name)
                    tm_block(wt, bwt, 0, 512, kcn, A, bA, t8, evac)
                mm_flush()
                if after_tile is not None:
                    after_tile(n)

        if stop >= 7:
            for tg in range(2):
                MXg = BIGA[:, 0:32768].rearrange("p (kc t) -> p kc t", kc=32)
                b_mxg = fresh(b_biga)
                P.dma(SP, MXg, MX.rearrange("(kc p) t -> p kc t", p=128)[:, :, tg * TG:(tg + 1) * TG], reads=(db("MX"),), writes=(b_mxg,))
                resid_gemm(MXg, b_mxg, 32, w_o, 0, x_own, None, X1, "X1", tg)
                retire(b_biga, b_mxg)

        if stop >= 8:
            for tg in range(2):
                hT = build_hT(X1, tg * TG, TG, GT2, b_gt2, (db("X1"),))
                for j in range(D_FF // 256):
                    wt, bwt = next_wt(32, 512)
                    wload(wt[:, :, 0:256], bwt, w_g, 0, 32, j * 256, 256)
                    wload(wt[:, :, 256:512], bwt, w_u, 0, 32, j * 256, 256, first=False)
                    def p6_unit(c, tb, j=j, tg=tg, wt=wt, bwt=bwt, hT=hT):
                        if True:
                            psg, bpsg = PSR.next()
                            psu, bpsu = PSR.next()
                            P.group(PE, [lambda e, kc=kc, psg=psg: e.matmul(psg, wt[:, kc, c * 128:(c + 1) * 128], hT[:, kc, tb * 512:(tb + 1) * 512],
                                                                            start=(kc == 0), stop=(kc == 31)) for kc in range(32)] +
                                        [lambda e, kc=kc, psu=psu: e.matmul(psu, wt[:, kc, 256 + c * 128:256 + (c + 1) * 128], hT[:, kc, tb * 512:(tb + 1) * 512],
                                                                            start=(kc == 0), stop=(kc == 31)) for kc in range(32)],
                                    reads=(bwt, b_biga), writes=(bpsg, bpsu))
                            sg, bsg = F32R.next()
                            hd, bhd = B16R.next()
                            P.op(ACT, lambda e, sg=sg, psg=psg: e.activation(out=sg, in_=psg, func=AF.Silu), reads=(bpsg,), writes=(bsg,))
                            P.op(DVE, lambda e, hd=hd, sg=sg, psu=psu: e.tensor_tensor(hd, psu, sg, ALU.mult), reads=(bpsu, bsg), writes=(bhd,))
                            r0, t0 = j * 256 + c * 128, tg * TG + tb * 512
                            store(HD[r0:r0 + 128, t0:t0 + 512], hd, bhd, "HD")
                    for c in range(2):
                        for tb in range(2):
                            p6_unit(c, tb)

        def final_block_slow(t8):
            yb, byb = XB[:, :], b_xb
            P.dma(ACT, yb, Y2[t8 * 128:(t8 + 1) * 128, :], reads=(db("Y2_0_2"),), writes=(byb,))
            (ss, bss), (rs, brs) = smallr.next(), smallr.next()
            P.op(DVE, lambda e: e.memset(ss, 0.0), writes=(bss,))
            P.op(ACT, lambda e: e.activation(out=XN[0][:, :], in_=yb, func=AF.Square, accum_out=ss), reads=(byb,), writes=(b_xn[0], bss))
            P.op(DVE, lambda e: e.tensor_scalar(out=rs, in0=ss, scalar1=1.0 / D, scalar2=EPS, op0=ALU.mult, op1=ALU.add), reads=(bss,), writes=(brs,))
            P.op(ACT, lambda e: e.activation(out=rs, in_=rs, func=AF.Sqrt), reads=(brs,), writes=(brs,))
            P.op(DVE, lambda e: e.reciprocal(rs, rs), reads=(brs,), writes=(brs,))
            fgh = XN[1][:, :].bitcast(F32)
            for hf in range(2):
                P.dma(ACT, fgh, fgv[:, hf * 2048:(hf + 1) * 2048].partition_broadcast(128).rearrange("p a b -> p (a b)"), writes=(b_xn[1],))
                P.op(DVE, lambda e, hf=hf: e.scalar_tensor_tensor(out=yb[:, hf * 2048:(hf + 1) * 2048], in0=yb[:, hf * 2048:(hf + 1) * 2048],
                                                                  scalar=rs, in1=fgh, op0=ALU.mult, op1=ALU.mult),
                     reads=(brs, b_xn[1]), part=(byb,))
            P.dma(ACT, OUT[t8 * 128:(t8 + 1) * 128, :], yb, reads=(byb,), part=(db("OUT"),))

        if stop >= 9:
            for tg in range(2):
                k0 = 0
                for ks, kcn in enumerate(KSPLIT):
                    HDh = BIGA[:, 0:kcn * TG].rearrange("p (kc t) -> p kc t", kc=kcn)
                    b_hdh = fresh(b_biga)
                    P.dma(SP, HDh, HD[k0:k0 + kcn * 128, tg * TG:(tg + 1) * TG].rearrange("(kc p) t -> p kc t", p=128),
                          reads=(db("HD"),), writes=(b_hdh,))
                    at = (lambda n: final_block_slow(n)) if (tg == 1 and ks == 0 and stop >= 10) else None
                    resid_gemm(HDh, b_hdh, kcn, w_d, k0, X1 if ks == 0 else Y2, "X1" if ks == 0 else f"Y2_{tg}_{ks - 1}", Y2, f"Y2_{tg}_{ks}", tg,
                               after_tile=at)
                    retire(b_biga, b_hdh)
                    k0 += kcn * 128

        if stop >= 10:
            FG = WT[0][:, 0:8192].bitcast(F32)
            b_fg = fresh(b_wt[0])
            P.dma(SP, FG, fgv.partition_broadcast(128).rearrange("p a b -> p (a b)"), writes=(b_fg,))
            ybs = [(XB[:, :], b_xb), (WT[1][:, 0:8192].bitcast(F32), b_wt[1])]
            for t8 in range(8, 16):
                yb, byb = ybs[t8 % 2]
                xn, bxn = XN[t8 % 2][:, :], b_xn[t8 % 2]
                P.dma(SP, yb, Y2[t8 * 128:(t8 + 1) * 128, :], reads=(db("Y2_1_2"),), writes=(byb,))
                ss, bss = smallr.next()
                rs, brs = smallr.next()
                P.op(DVE, lambda e, ss=ss: e.memset(ss, 0.0), writes=(bss,))
                P.op(ACT, lambda e, xn=xn, yb=yb, ss=ss: e.activation(out=xn, in_=yb, func=AF.Square, accum_out=ss),
                     reads=(byb,), writes=(bxn, bss))
                P.op(DVE, lambda e, rs=rs, ss=ss: e.tensor_scalar(out=rs, in0=ss, scalar1=1.0 / D, scalar2=EPS,
                                                                  op0=ALU.mult, op1=ALU.add), reads=(bss,), writes=(brs,))
                P.op(ACT, lambda e, rs=rs: e.activation(out=rs, in_=rs, func=AF.Sqrt), reads=(brs,), writes=(brs,))
                P.op(DVE, lambda e, rs=rs: e.reciprocal(rs, rs), reads=(brs,), writes=(brs,))
                P.op(DVE, lambda e, yb=yb, rs=rs: e.scalar_tensor_tensor(out=yb, in0=yb, scalar=rs, in1=FG, op0=ALU.mult, op1=ALU.mult),
                     reads=(brs, b_fg), writes=(byb,))
                P.dma(SP, OUT[t8 * 128:(t8 + 1) * 128, :], yb, reads=(byb,), part=(db("OUT"),))

        for q in (SP, ACT, POOL):
            for sl in q.slots:
                SP.wait(sl, sl.v)

        global LAST_ENGS
        LAST_ENGS = [PE, ACT, DVE, POOL, SP]
        block.tensor(lambda e: _emit(e, PE.ops))
        block.scalar(lambda e: _emit(e, ACT.ops))
        block.vector(lambda e: _emit(e, DVE.ops))
        block.gpsimd(lambda e: _emit(e, POOL.ops))
        block.sync(lambda e: _emit(e, SP.ops))
    return nc


def _t5_bucket(dist):
    nb, md = 32, 128
    max_exact = nb // 2
    small = dist < max_exact
    large = max_exact + (np.log(np.maximum(dist, 1) / max_exact) / np.log(md / max_exact) * (nb - max_exact)).astype(np.int64)
    large = np.minimum(large, nb - 1)
    return np.where(small, dist, large)


def _consts(rel_bias):
    k = np.arange(128)[:, None]
    q = np.arange(128)[None, :]
    cst = np.zeros((128, 640), np.float32)
    cst[:, 0:128] = np.eye(128, dtype=np.float32)
    cst[:, 128:256] = (k <= q)
    cst[:, 256:384] = np.where(k <= q, 0.0, NEG)
    cst[:, 384:512] = np.where(k > q, 0.0, NEG)
    cst[:, 512:640] = np.where(k <= q, 0.0, NEG)
    bprev = _t5_bucket(np.clip(q + 128 - k, 0, None))
    bcur = _t5_bucket(np.clip(q - k, 0, None))
    braw = np.empty((128, 2, 32, 128), np.float32)
    braw[:, 0] = np.transpose(rel_bias[bprev], (0, 2, 1))
    braw[:, 1] = np.transpose(rel_bias[bcur], (0, 2, 1))
    e2 = np.zeros((64, 32, 128), np.float32)
    for r in range(64):
        e2[r, r % 32, :] = 1.0
    return cst, np.ascontiguousarray(braw.reshape(128, -1)), e2.reshape(64, 4096)


def prep_core(inputs, c, shared):
    x = inputs["x"]
    b, par = c // 2, c % 2
    own = [2 * s + par for s in range(4)]
    forn = [2 * s + (1 - par) for s in range(4)]
    x_own = np.concatenate([x[b, sb * 512:(sb + 1) * 512] for sb in own], axis=0)
    x_for = np.concatenate([x[b, sb * 512:(sb + 1) * 512] for sb in forn], axis=0)
    halo = []
    for sb in own:
        halo.append(x[b, sb * 512 - 128: sb * 512] if sb > 0 else np.zeros((128, D), np.float32))
    pc = np.zeros((128, 8), np.float32)
    pc[:, 0] = float(par)
    pc[:, 1] = 1.0 - float(par)
    pc[:, 2] = NEG if par == 0 else 0.0
    for s, sb in enumerate(own):
        pc[:, 3 + s] = NEG if sb == 0 else 0.0
    m = dict(shared)
    m.update(x_own=np.ascontiguousarray(x_own), x_for=np.ascontiguousarray(x_for),
             x_halo=np.ascontiguousarray(np.concatenate(halo, axis=0)), pcore=pc)
    return m, own


def prep_shared(inputs):
    f = lambda a: np.ascontiguousarray(np.asarray(a, dtype=np.float32))
    cst, braw, e2 = _consts(f(inputs["rel_bias"]))
    return dict(
        n1g=f(np.asarray(inputs["norm1_g"])[0].reshape(KC, 128).T),
        n2g=f(np.asarray(inputs["norm2_g"])[0].reshape(KC, 128).T),
        fgv=f(np.asarray(inputs["final_g"]).reshape(1, D)),
        w_in=f(inputs["w_in"][0]), bfg=f(np.asarray(inputs["b_forget"]).reshape(1, 16)),
        sinks=f(np.asarray(inputs["attn_sinks"]).reshape(32, 1)), bias_raw=braw,
        w_a=f(inputs["w_branch_a"][0]), w_b=f(inputs["w_branch_b"][0]), w_o=f(inputs["w_out"][0]),
        w_g=f(inputs["w_ffn_gate"][0]), w_u=f(inputs["w_ffn_up"][0]), w_d=f(inputs["w_ffn_down"][0]),
        cst=cst, e2c=e2)


def kernel(**inputs):
    inputs = {k: np.asarray(v) for k, v in inputs.items()}
    shared = prep_shared(inputs)
    maps, owns = [], []
    for c in range(8):
        m, own = prep_core(inputs, c, shared)
        maps.append(m)
        owns.append(own)
    nc = build()
    res = run_bass_kernel_spmd(nc, maps, core_ids=list(range(8)))
    out = np.empty((4, SEQ, D), np.float32)
    for c in range(8):
        o = res.results[c]["out"]
        for s, sb in enumerate(owns[c]):
            out[c // 2, sb * 512:(sb + 1) * 512] = o[s * 512:(s + 1) * 512]
    return out
```

```python
import numpy as np
from contextlib import ExitStack
import concourse.bass as bass
import concourse.mybir as mybir
from concourse.bass_utils import run_bass_kernel_spmd

F32 = mybir.dt.float32
BF16 = mybir.dt.bfloat16
AF = mybir.ActivationFunctionType
ALU = mybir.AluOpType

D = 4096
KC = 32
SEQ = 4096
H_A, DH_A = 16, 128
H_B, HKV_B, DH_B = 32, 4, 64
D_FF = 11008
W_IN = 16912
C_QA, C_KA, C_VA, C_FA, C_QB, C_KB, C_VB, C_GA, C_GB = 0, 2048, 4096, 6144, 6160, 8208, 8464, 8720, 12816
TOWN = 2048
TG = 1024
NEG = -30000.0
EPS = 1e-6
KSPLIT = [29, 29, 28]


class SemC:
    def __init__(self, h):
        self.h = h
        self.v = 0


class Buf:
    __slots__ = ("w", "r")

    def __init__(self):
        self.w = {}
        self.r = {}


def _merge(d, sc, v):
    cur = d.get(id(sc))
    if cur is None or cur[1] < v:
        d[id(sc)] = (sc, v)


class Eng:
    def __init__(self, name, sem, slots, skip_self=False):
        self.name, self.sem, self.slots, self.skip_self = name, sem, slots, skip_self
        self.si = 0
        self.ops = []
        self.waited = {}

    def wait(self, sc, v):
        if v <= 0 or self.waited.get(id(sc), 0) >= v:
            return
        self.waited[id(sc)] = v
        self.ops.append(("w", sc.h, v))


class Prog:
    def __init__(self, nc, es):
        self.nc = nc
        mk = lambda n: SemC(es.enter_context(nc.semaphore(n)))
        self.PE = Eng("pe", mk("c_pe"), [], skip_self=True)
        self.ACT = Eng("act", mk("c_act"), [mk(f"qa{i}") for i in range(4)])
        self.DVE = Eng("dve", mk("c_dve"), [])
        self.POOL = Eng("pool", mk("c_pool"), [mk(f"qg{i}") for i in range(6)])
        self.SP = Eng("sp", mk("c_sp"), [mk(f"qs{i}") for i in range(28)])
        self.flip = 0

    def _deps(self, eng, reads, writes, part):
        d = {}
        for b in reads:
            for sc, v in b.w.values():
                _merge(d, sc, v)
        for b in writes:
            for sc, v in b.w.values():
                _merge(d, sc, v)
            for sc, v in b.r.values():
                _merge(d, sc, v)
        for b in part:
            for sc, v in b.r.values():
                _merge(d, sc, v)
        for sc, v in d.values():
            if eng.skip_self and sc is eng.sem:
                continue
            eng.wait(sc, v)

    def _commit(self, sc, v, reads, writes, part):
        for b in reads:
            _merge(b.r, sc, v)
        for b in writes:
            b.w = {id(sc): (sc, v)}
            b.r = {}
        for b in part:
            _merge(b.w, sc, v)

    def op(self, eng, fn, reads=(), writes=(), part=()):
        self._deps(eng, reads, writes, part)
        eng.sem.v += 1
        eng.ops.append(("i", fn, eng.sem.h, 1))
        self._commit(eng.sem, eng.sem.v, reads, writes, part)

    def group(self, eng, fns, reads=(), writes=(), part=()):
        self._deps(eng, reads, writes, part)
        for f in fns[:-1]:
            eng.ops.append(("n", f))
        eng.sem.v += 1
        eng.ops.append(("i", fns[-1], eng.sem.h, 1))
        self._commit(eng.sem, eng.sem.v, reads, writes, part)

    def dma(self, q, out, in_, reads=(), writes=(), part=(), slow=False):
        sl = q.slots[q.si]
        q.si = (q.si + 1) % len(q.slots)
        q.wait(sl, sl.v)
        self._deps(q, reads, writes, part)
        sl.v += 16
        if slow:
            fn = lambda e, o=out, i=in_: e.dma_start(out=o, in_=i, allow_slow_non_contiguous=True)
        else:
            fn = lambda e, o=out, i=in_: e.dma_start(out=o, in_=i)
        q.ops.append(("i", fn, sl.h, 16))
        self._commit(sl, sl.v, reads, writes, part)

    def ev_engine(self):
        self.flip ^= 1
        return self.ACT if self.flip else self.DVE


def _emit(e, ops):
    for o in ops:
        if o[0] == "w":
            e.wait_ge(o[1], o[2])
        elif o[0] == "n":
            o[1](e)
        else:
            o[1](e).then_inc(o[2], o[3])


class Ring:
    def __init__(self, aps):
        self.items = [(ap, Buf()) for ap in aps]
        self.i = 0

    def next(self):
        it = self.items[self.i % len(self.items)]
        self.i += 1
        return it


def build(debug_out=(), stop=99):
    nc = bass.Bass("TRN2", target_bir_lowering=False)

    def din(name, shape):
        return nc.dram_tensor(name, list(shape), F32, kind="ExternalInput").ap()

    _din = din

    def din(name, shape):
        if stop <= 0 and name.startswith("w_"):
            shape = [128, 128]
        return _din(name, shape)

    def dsc(name, shape, dt):
        kind = "ExternalOutput" if name in debug_out else "Internal"
        return nc.dram_tensor(name, list(shape), dt, kind=kind).ap()

    x_own = din("x_own", [TOWN, D])
    x_for = din("x_for", [TOWN, D])
    x_halo = din("x_halo", [512, D])
    n1g = din("n1g", [128, KC])
    n2g = din("n2g", [128, KC])
    fgv = din("fgv", [1, D])
    w_in = din("w_in", [D, W_IN])
    bfg = din("bfg", [1, 16])
    sinks = din("sinks", [32, 1])
    bias_raw = din("bias_raw", [128, 2 * 32 * 128])
    w_a = din("w_a", [2048, D])
    w_b = din("w_b", [2048, D])
    w_o = din("w_o", [D, D])
    w_g = din("w_g", [D, D_FF])
    w_u = din("w_u", [D, D_FF])
    w_d = din("w_d", [D_FF, D])
    cst = din("cst", [128, 640])
    e2c = din("e2c", [64, 4096])
    pcore = din("pcore", [128, 8])
    OUT = nc.dram_tensor("out", [TOWN, D], F32, kind="ExternalOutput").ap()

    QA = dsc("QA", [2048, TOWN], BF16)
    KA = dsc("KA", [2048, 4096], BF16)
    VA = dsc("VA", [4096, 2048], BF16)
    QB = dsc("QB", [2048, TOWN], BF16)
    KB = dsc("KB", [256, 2560], BF16)
    VB = dsc("VB", [2560, 256], BF16)
    GA = dsc("GA", [D, TOWN], BF16)
    GB = dsc("GB", [D, TOWN], BF16)
    OA = dsc("OA", [2048, TOWN], BF16)
    OB = dsc("OB", [2048, TOWN], BF16)
    MX = dsc("MX", [D, TOWN], BF16)
    X1 = dsc("X1", [TOWN, D], F32)
    HD = dsc("HD", [D_FF, TOWN], BF16)
    Y2 = dsc("Y2", [TOWN, D], F32)
    CDBG = dsc("CDBG", [128, 512], F32)
    dram_bufs = {}

    def db(name):
        if name not in dram_bufs:
            dram_bufs[name] = Buf()
        return dram_bufs[name]

    es = ExitStack()
    with es:
        P = Prog(nc, es)
        PE, ACT, DVE, POOL, SP = P.PE, P.ACT, P.DVE, P.POOL, P.SP
        sb = lambda n, shape, dt: es.enter_context(nc.sbuf_tensor(n, list(shape), dt))
        BIGA = sb("BIGA", [128, 32768], BF16)
        WT = [sb(f"WT{i}", [128, 16384], BF16) for i in range(2)]
        XB = sb("XB", [128, 4096], F32)
        XN = [sb(f"XN{i}", [128, 4096], BF16) for i in range(2)]
        f32pool = [sb(f"f32p{i}", [128, 512], F32) for i in range(8)]
        b16pool = [sb(f"b16p{i}", [128, 512], BF16) for i in range(8)]
        CST = sb("CST", [128, 640], F32)
        IDB = sb("IDB", [128, 128], BF16)
        TRIB = sb("TRIB", [128, 128], BF16)
        ONESF = sb("ONESF", [128, 128], F32)
        ONESB = sb("ONESB", [128, 128], BF16)
        PCORE = sb("PCORE", [128, 8], F32)
        GT1 = sb("GT1", [128, KC], F32)
        GT2 = sb("GT2", [128, KC], F32)
        BFB = sb("BFB", [128, 16], F32)
        LF = sb("LF", [128, 512], F32)
        CC = sb("CC", [128, 512], F32)
        NEGC = sb("NEGC", [128, 512], F32)
        TOT = sb("TOT", [128, 512], F32)
        OFFB = sb("OFFB", [128, 512], F32)
        SBT = sb("SBT", [128, 8 * 16], F32)
        OFS = sb("OFS", [128, 8 * 16], F32)
        PFX = sb("PFX", [128, 4 * 16], F32)
        SMALL = sb("SMALL", [128, 64], F32)
        FTMP = sb("FTMP", [128, 6 * 16], F32)
        ESK = sb("ESK", [64, 8], F32)
        ESKB = sb("ESKB", [64, 8], BF16)
        ESL2 = sb("ESL2", [128, 128], BF16)
        PSB = [es.enter_context(nc.psum_tensor(f"ps{i}", [128, 512], F32)) for i in range(8)]
        block = es.enter_context(nc.Block())

        ident_f = CST[:, 0:128]
        utri_f = CST[:, 128:256]
        trimask = CST[:, 256:384]
        b_cst, b_idb, b_onesf, b_onesb, b_pcore = Buf(), Buf(), Buf(), Buf(), Buf()
        b_gt1, b_gt2, b_bfb = Buf(), Buf(), Buf()
        b_biga, b_wt = Buf(), [Buf(), Buf()]
        b_xb, b_xn = Buf(), [Buf(), Buf()]
        F32R = Ring([t[:, :] for t in f32pool])
        B16R = Ring([t[:, :] for t in b16pool])
        PSR = Ring([t[:, :] for t in PSB])
        smallr = Ring([SMALL[:, i:i + 1] for i in range(32)])

        P.dma(SP, CST[:, :], cst, writes=(b_cst,))
        P.dma(SP, PCORE[:, :], pcore, writes=(b_pcore,))
        P.dma(SP, GT1[:, :], n1g, writes=(b_gt1,))
        P.dma(SP, GT2[:, :], n2g, writes=(b_gt2,))
        P.dma(SP, BFB[:, :], bfg.partition_broadcast(128).rearrange("p a b -> p (a b)"), writes=(b_bfb,))
        P.dma(POOL, IDB[:, :], cst[:, 0:128], writes=(b_idb,))
        P.op(POOL, lambda e: e.memset(ONESF[:, :], 1.0), writes=(b_onesf,))
        P.op(POOL, lambda e: e.memset(ONESB[:, :], 1.0), writes=(b_onesb,))
        ALPHA = PCORE[:, 0:1]
        ALPHA1 = PCORE[:, 1:2]
        PEN = PCORE[:, 2:3]

        def build_hT(xsrc, row0, T, gt, b_gt, xsrc_bufs):
            hT = BIGA[:, 0:KC * T].rearrange("p (kc t) -> p kc t", kc=KC)
            import os
            nblk = int(os.environ.get('NBLK_DBG', T // 128))
            jmax = int(os.environ.get('JMAX_DBG', 4))
            noev = int(os.environ.get('NOEV_DBG', 0))
            xbs = [(XB[:, :], b_xb), (WT[1][:, 0:8192].bitcast(F32), b_wt[1])]
            sm = [(smallr.next(), smallr.next()) for _ in range(nblk)]
            for (ss, bss), _ in sm:
                P.op(DVE, lambda e, ss=ss: e.memset(ss, 0.0), writes=(bss,))
            for tb in range(nblk):
                xb, bxb = xbs[tb % 2]
                xn, bxn = XN[tb % 2][:, :], b_xn[tb % 2]
                P.dma(SP, xb, xsrc[row0 + tb * 128: row0 + (tb + 1) * 128, :], reads=xsrc_bufs, writes=(bxb,))
                (ss, bss), (rs, brs) = sm[tb]
                P.op(ACT, lambda e, xn=xn, xb=xb, ss=ss: e.activation(out=xn, in_=xb, func=AF.Square, accum_out=ss),
                     reads=(bxb,), writes=(bxn, bss))
                P.op(DVE, lambda e, rs=rs, ss=ss: e.tensor_scalar(out=rs, in0=ss, scalar1=1.0 / D, scalar2=EPS,
                                                                  op0=ALU.mult, op1=ALU.add), reads=(bss,), writes=(brs,))
                P.op(ACT, lambda e, rs=rs: e.activation(out=rs, in_=rs, func=AF.Sqrt), reads=(brs,), writes=(brs,))
                P.op(DVE, lambda e, rs=rs: e.reciprocal(rs, rs), reads=(brs,), writes=(brs,))
                P.op(ACT, lambda e, xn=xn, xb=xb, rs=rs: e.activation(out=xn[:, 0:1536], in_=xb[:, 0:1536], func=AF.Copy, scale=rs),
                     reads=(bxb, brs), writes=(bxn,))
                P.op(DVE, lambda e, xn=xn, xb=xb, rs=rs: e.tensor_scalar(out=xn[:, 1536:4096], in0=xb[:, 1536:4096], scalar1=rs, scalar2=0.0,
                                                                          op0=ALU.mult, op1=ALU.add),
                     reads=(bxb, brs), part=(bxn,))
                for j in range(4):
                    ps, bps = PSR.next()
                    pv = ps.bitcast(BF16).rearrange("p (c t) -> p c t", c=8)
                    fns = [lambda e, o=pv[:, c, :], i=xn[:, (j * 8 + c) * 128:(j * 8 + c + 1) * 128]:
                           e.transpose(out=o, in_=i, identity=IDB[:, :]) for c in range(8)]
                    P.group(PE, fns, reads=(bxn, b_idb), writes=(bps,))
                    dst = hT[:, j * 8:(j + 1) * 8, tb * 128:(tb + 1) * 128]
                    gsc = gt[:, j * 8:(j + 1) * 8].unsqueeze(2).to_broadcast([128, 8, 128])
                    P.op(DVE, lambda e, o=dst, i=pv, g_=gsc: e.tensor_tensor(o, i, g_, ALU.mult), reads=(bps, b_gt), part=(b_biga,))
            return hT

        wt_i = [0]

        def next_wt(kcn, width):
            i = wt_i[0] % 2
            wt_i[0] += 1
            v = WT[i][:, 0:kcn * width].rearrange("p (kc n) -> p kc n", kc=kcn)
            return v, b_wt[i]

        def wload(dst, bdst, src, k0, kcn, c0, ncols, first=True):
            srcv = src[k0:k0 + kcn * 128, c0:c0 + ncols].rearrange("(kc p) n -> p kc n", p=128)
            if first:
                P.dma(POOL, dst, srcv, writes=(bdst,))
            else:
                P.dma(POOL, dst, srcv, part=(bdst,))

        def store(dst, src, bsrc, dname):
            P.dma(SP, dst, src, reads=(bsrc,), part=(db(dname),))

        def evac_copy(ps, bps, scale, func=None):
            st, bst = B16R.next()
            st = st[:, 0:ps.shape[-1]]
            if func is not None:
                P.op(ACT, lambda e, o=st, i=ps, f=func: e.activation(out=o, in_=i, func=f), reads=(bps,), writes=(bst,))
                return st, bst
            eng = P.ev_engine()
            if eng is ACT:
                P.op(ACT, lambda e, o=st, i=ps, s=scale: e.activation(out=o, in_=i, func=AF.Copy, scale=float(s)),
                     reads=(bps,), writes=(bst,))
            else:
                P.op(DVE, lambda e, o=st, i=ps, s=scale: e.tensor_scalar_mul(o, i, float(s)), reads=(bps,), writes=(bst,))
            return st, bst

        pend = []

        def mm_flush():
            if not pend:
                return
            fns, reads, seen = [], [], set()
            for p in pend:
                fns += p[0]
                for b in p[1]:
                    if id(b) not in seen:
                        seen.add(id(b))
                        reads.append(b)
            P.group(PE, fns, reads=tuple(reads), writes=tuple(p[3] for p in pend))
            for p in list(pend):
                p[4](p[2], p[3])
            pend.clear()

        def mm_submit(fns, reads, ps, bps, evac):
            pend.append((fns, reads, ps, bps, evac))
            if len(pend) == 2:
                mm_flush()

        def fm_block(wt, bwt, col, kcn, A, bA, tb, evac):
            ps, bps = PSR.next()
            fns = [lambda e, o=ps, l=wt[:, kc, col:col + 128], r=A[:, kc, tb * 512:(tb + 1) * 512], kc=kc:
                   e.matmul(o, l, r, start=(kc == 0), stop=(kc == kcn - 1)) for kc in range(kcn)]
            mm_submit(fns, (bwt, bA), ps, bps, evac)

        def tm_block(wt, bwt, col, ncols, kcn, A, bA, tb128, evac):
            ps, bps = PSR.next()
            fns = [lambda e, o=ps[:, 0:ncols], l=A[:, kc, tb128 * 128:(tb128 + 1) * 128], r=wt[:, kc, col:col + ncols], kc=kc:
                   e.matmul(o, l, r, start=(kc == 0), stop=(kc == kcn - 1)) for kc in range(kcn)]
            mm_submit(fns, (bwt, bA), ps, bps, evac)

        def phase1_group(kind, xsrc, row0, T, tok0):
            hT = build_hT(xsrc, row0, T, GT1, b_gt1, ())
            ntb = T // 512
            ktok0 = tok0 if kind == "own" else 2048 + tok0

            def fm_tile(c0, ncols, dst, drow0, dtok0, dname, scale=1.0, func=None):
                wt, bwt = next_wt(KC, ncols)
                wload(wt, bwt, w_in, 0, KC, c0, ncols)
                for c in range(ncols // 128):
                    for tb in range(ntb):
                        def evac(ps, bps, c=c, tb=tb):
                            st, bst = evac_copy(ps, bps, scale, func)
                            store(dst[drow0 + c * 128: drow0 + (c + 1) * 128, dtok0 + tb * 512: dtok0 + (tb + 1) * 512],
                                  st, bst, dname)
                        fm_block(wt, bwt, c * 128, KC, hT, b_biga, tb, evac)
                mm_flush()

            def tm_tile(c0, ncols, dst, dtok0, dcol0, dname):
                wt, bwt = next_wt(KC, ncols)
                wload(wt, bwt, w_in, 0, KC, c0, ncols)
                for t8 in range(T // 128):
                    def evac(ps, bps, t8=t8):
                        st, bst = evac_copy(ps[:, 0:ncols], bps, 1.0)
                        st = st[:, 0:ncols]
                        store(dst[dtok0 + t8 * 128: dtok0 + (t8 + 1) * 128, dcol0:dcol0 + ncols], st, bst, dname)
                    tm_block(wt, bwt, 0, ncols, KC, hT, b_biga, t8, evac)
                mm_flush()

            if kind == "own":
                for i in range(4):
                    fm_tile(C_QA + i * 512, 512, QA, i * 512, tok0, "QA", scale=DH_A ** -0.5)
            if kind in ("own", "for"):
                for i in range(4):
                    fm_tile(C_KA + i * 512, 512, KA, i * 512, ktok0, "KA")
                for i in range(4):
                    tm_tile(C_VA + i * 512, 512, VA, ktok0, i * 512, "VA")
                wt, bwt = next_wt(KC, 16)
                wload(wt, bwt, w_in, 0, KC, C_FA, 16)
                for t8 in range(T // 128):
                    blk = ktok0 // 128 + t8
                    def evac(ps, bps, blk=blk):
                        ft = FTMP[:, :].rearrange("p (a b) -> p a b", a=6)
                        bft = db("FTMP")
                        P.op(DVE, lambda e: e.tensor_tensor(ft[:, 0, :], ps[:, 0:16], BFB[:, :], ALU.add),
                             reads=(bps, b_bfb), writes=(bft,))
                        P.op(ACT, lambda e: e.activation(out=ft[:, 1, :], in_=ft[:, 0, :], func=AF.Abs),
                             reads=(bft,), part=(bft,))
                        P.op(ACT, lambda e: e.activation(out=ft[:, 2, :], in_=ft[:, 1, :], func=AF.Exp, scale=-1.0),
                             reads=(bft,), part=(bft,))
                        P.op(ACT, lambda e: e.activation(out=ft[:, 3, :], in_=ft[:, 2, :], func=AF.Ln, bias=1.0),
                             reads=(bft,), part=(bft,))
                        P.op(DVE, lambda e: e.tensor_scalar_min(ft[:, 4, :], ft[:, 0, :], 0.0), reads=(bft,), part=(bft,))
                        P.op(DVE, lambda e, blk=blk: e.tensor_sub(LF[:, blk * 16:(blk + 1) * 16], ft[:, 4, :], ft[:, 3, :]),
                             reads=(bft,), part=(db("LF"),))
                    tm_block(wt, bwt, 0, 16, KC, hT, b_biga, t8, evac)
                    mm_flush()
            if kind == "own":
                for i in range(4):
                    fm_tile(C_QB + i * 512, 512, QB, i * 512, tok0, "QB", scale=DH_B ** -0.5)
            if kind in ("own", "halo"):
                btok0 = tok0 if kind == "own" else 2048 + tok0
                wt, bwt = next_wt(KC, 512)
                wload(wt, bwt, w_in, 0, KC, C_KB, 512)
                for c in range(2):
                    for tb in range(ntb):
                        def evac(ps, bps, c=c, tb=tb):
                            st, bst = evac_copy(ps, bps, 1.0)
                            store(KB[c * 128:(c + 1) * 128, btok0 + tb * 512: btok0 + (tb + 1) * 512], st, bst, "KB")
                        fm_block(wt, bwt, c * 128, KC, hT, b_biga, tb, evac)
                mm_flush()
                for t8 in range(T // 128):
                    def evac(ps, bps, t8=t8):
                        st, bst = evac_copy(ps[:, 0:256], bps, 1.0)
                        store(VB[btok0 + t8 * 128: btok0 + (t8 + 1) * 128, :], st[:, 0:256], bst, "VB")
                    tm_block(wt, bwt, 256, 256, KC, hT, b_biga, t8, evac)
                mm_flush()
            if kind == "own":
                for i in range(8):
                    fm_tile(C_GA + i * 512, 512, GA, i * 512, tok0, "GA", func=AF.Sigmoid)
                for i in range(8):
                    fm_tile(C_GB + i * 512, 512, GB, i * 512, tok0, "GB", func=AF.Sigmoid)

        if stop == 0:
            build_hT(x_own, 0, TG, GT1, b_gt1, ())
        if stop >= 1:
            phase1_group("own", x_own, 0, TG, 0)
        if stop >= 2:
            phase1_group("own", x_own, TG, TG, TG)
            phase1_group("for", x_for, 0, TG, 0)
            phase1_group("for", x_for, TG, TG, TG)
            phase1_group("halo", x_halo, 0, 512, 0)

        def retire(into, *bufs):
            for b in bufs:
                for sc, v in list(b.r.values()) + list(b.w.values()):
                    _merge(into.r, sc, v)

        def fresh(*srcs):
            nb = Buf()
            for b in srcs:
                for sc, v in list(b.r.values()) + list(b.w.values()):
                    _merge(nb.r, sc, v)
            return nb

        if stop >= 3:
            b_lf = db("LF")
            b_cc, b_negc, b_tot, b_offb, b_sbt, b_ofs, b_pfx = Buf(), Buf(), Buf(), Buf(), Buf(), Buf(), Buf()
            psc, bpsc = PSR.next()
            pst, bpst = PSR.next()
            P.group(PE, [lambda e: e.matmul(psc, utri_f, LF[:, :], start=True, stop=True)], reads=(b_cst, b_lf), writes=(bpsc,))
            P.group(PE, [lambda e: e.matmul(pst, ONESF[:, :], LF[:, :], start=True, stop=True)], reads=(b_onesf, b_lf), writes=(bpst,))
            P.op(DVE, lambda e: e.tensor_copy(TOT[:, :], pst), reads=(bpst,), writes=(b_tot,))
            tot4 = TOT[:, :].rearrange("p (s r h) -> p s r h", s=8, r=4)
            sbt3 = SBT[:, :].rearrange("p (s h) -> p s h", s=8)
            ofs3 = OFS[:, :].rearrange("p (s h) -> p s h", s=8)
            pfx3 = PFX[:, :].rearrange("p (s h) -> p s h", s=4)
            offb4 = OFFB[:, :].rearrange("p (s r h) -> p s r h", s=8, r=4)
            P.op(DVE, lambda e: e.tensor_add(sbt3, tot4[:, :, 0, :], tot4[:, :, 1, :]), reads=(b_tot,), writes=(b_sbt,))
            P.op(DVE, lambda e: e.tensor_add(sbt3, sbt3, tot4[:, :, 2, :]), reads=(b_tot, b_sbt), writes=(b_sbt,))
            P.op(DVE, lambda e: e.tensor_add(sbt3, sbt3, tot4[:, :, 3, :]), reads=(b_tot, b_sbt), writes=(b_sbt,))
            P.op(DVE, lambda e: e.memset(PFX[:, 0:16], 0.0), writes=(b_pfx,))
            for s in range(1, 4):
                P.op(DVE, lambda e, s=s: e.tensor_add(pfx3[:, s, :], pfx3[:, s - 1, :], sbt3[:, s - 1, :]),
                     reads=(b_sbt, b_pfx), writes=(b_pfx,))
                P.op(DVE, lambda e, s=s: e.tensor_add(pfx3[:, s, :], pfx3[:, s, :], sbt3[:, 4 + s - 1, :]),
                     reads=(b_sbt, b_pfx), writes=(b_pfx,))
            for s in range(4):
                P.op(DVE, lambda e, s=s: e.scalar_tensor_tensor(out=ofs3[:, s, :], in0=sbt3[:, 4 + s, :], scalar=ALPHA,
                                                                in1=pfx3[:, s, :], op0=ALU.mult, op1=ALU.add),
                     reads=(b_sbt, b_pfx, b_pcore), writes=(b_ofs,))
                P.op(DVE, lambda e, s=s: e.scalar_tensor_tensor(out=ofs3[:, 4 + s, :], in0=sbt3[:, s, :], scalar=ALPHA1,
                                                                in1=pfx3[:, s, :], op0=ALU.mult, op1=ALU.add),
                     reads=(b_sbt, b_pfx, b_pcore, b_ofs), writes=(b_ofs,))
            P.op(DVE, lambda e: e.tensor_copy(offb4[:, :, 0, :], ofs3), reads=(b_ofs,), writes=(b_offb,))
            for r in range(1, 4):
                P.op(DVE, lambda e, r=r: e.tensor_add(offb4[:, :, r, :], offb4[:, :, r - 1, :], tot4[:, :, r - 1, :]),
                     reads=(b_tot, b_offb), writes=(b_offb,))
            P.op(DVE, lambda e: e.tensor_tensor(CC[:, :], psc, OFFB[:, :], ALU.add), reads=(bpsc, b_offb), writes=(b_cc,))
            P.op(DVE, lambda e: e.tensor_scalar_mul(NEGC[:, :], CC[:, :], -1.0), reads=(b_cc,), writes=(b_negc,))
            if "CDBG" in debug_out:
                P.dma(SP, CDBG, CC[:, :], reads=(b_cc,), part=(db("CDBG"),))

        if stop >= 4:
            QTv = [BIGA[:, i * 2048:(i + 1) * 2048] for i in range(2)]
            KTv = [BIGA[:, 4096 + i * 4096: 4096 + (i + 1) * 4096] for i in range(2)]
            VHv = [BIGA[:, 12288 + i * 4096: 12288 + (i + 1) * 4096].rearrange("p (b d) -> p b d", b=32) for i in range(2)]
            W1 = WT[1]
            f32v = lambda lo, n: W1[:, lo:lo + 2 * n].bitcast(F32)
            DGv = [f32v(0, 128), f32v(256, 128), f32v(512, 128), f32v(768, 128)]
            A0, A1, A2 = f32v(1024, 512), f32v(2048, 512), f32v(3072, 512)
            HH, MM, LL = W1[:, 4096:4608], W1[:, 4608:5120], W1[:, 5120:5632]
            CQ3 = [W1[:, 6144:6656], W1[:, 6656:7168]]
            BK = [f32v(8192, 32), f32v(8256, 32)]
            BKP = [f32v(8320, 4), f32v(8328, 4)]
            RCOL = [f32v(8336, 1), f32v(8338, 1)]
            b_qt, b_kt, b_vh = [Buf(), Buf()], [Buf(), Buf()], [Buf(), Buf()]
            for b in b_qt + b_kt + b_vh:
                b.r = dict(b_biga.r)
                b.r.update(b_biga.w)
            mkw1 = lambda: fresh(b_wt[1])
            b_dg = [mkw1(), mkw1(), mkw1(), mkw1()]
            b_a0, b_a1, b_a2, b_hh, b_mm, b_ll = mkw1(), mkw1(), mkw1(), mkw1(), mkw1(), mkw1()
            b_cq3, b_bk, b_bkp, b_rcol = [mkw1(), mkw1()], [mkw1(), mkw1()], [mkw1(), mkw1()], [mkw1(), mkw1()]
            b_trib = Buf()
            P.dma(POOL, TRIB[:, :], cst[:, 256:384], writes=(b_trib,))
            for i in range(2):
                P.op(POOL, lambda e, i=i: e.memset(CQ3[i], 0.0), writes=(b_cq3[i],))
            VAv = VA.rearrange("(b p) c -> p b c", p=128)
            S_R = Ring([PSB[0][:, :], PSB[1][:, :], PSB[2][:, :]])
            OT_R = Ring([PSB[3][:, :], PSB[4][:, :]])
            RS_R = Ring([PSB[5][:, :], PSB[6][:, :]])
            psCQ, b_pscq = PSB[7][:, :], Buf()
            CC3 = CC[:, :].rearrange("p (b h) -> p b h", h=16)
            dgi = [0]

            def fox_head_loads(h):
                hb = h % 2
                P.dma(SP, QTv[hb], QA[h * 128:(h + 1) * 128, :], reads=(db("QA"),), writes=(b_qt[hb],))
                P.dma(SP, KTv[hb], KA[h * 128:(h + 1) * 128, :], reads=(db("KA"),), writes=(b_kt[hb],))
                P.dma(SP, VHv[hb], VAv[:, :, h * 128:(h + 1) * 128], reads=(db("VA"),), writes=(b_vh[hb],))

            def fox_prep_a(h, s):
                for r in range(4):
                    blk = 4 * s + r
                    P.op(DVE, lambda e, r=r, blk=blk: e.tensor_scalar_mul(DGv[r], ident_f, CC[:, blk * 16 + h: blk * 16 + h + 1]),
                         reads=(b_cst, b_cc), writes=(b_dg[r],))

            def fox_prep(h, s):
                pb = (h * 4 + s) % 2
                for r in range(4):
                    P.group(PE, [lambda e, r=r: e.matmul(psCQ[:, r * 128:(r + 1) * 128], ONESF[:, :], DGv[r], start=True, stop=True)],
                            reads=(b_onesf, b_dg[r]), writes=(b_pscq,) if r == 0 else (), part=() if r == 0 else (b_pscq,))
                rc_, bk_, bkp_, cq_ = RCOL[pb], BK[pb], BKP[pb], CQ3[pb]
                P.op(DVE, lambda e: e.tensor_copy(rc_, psCQ[:, 0:1]), reads=(b_pscq,), writes=(b_rcol[pb],))
                P.op(DVE, lambda e: e.tensor_scalar(out=A0[0:65, :], in0=psCQ[0:65, :], scalar1=rc_[0:65, :], scalar2=0.0, op0=ALU.subtract, op1=ALU.add),
                     reads=(b_pscq, b_rcol[pb]), writes=(b_a0,))
                P.op(DVE, lambda e: e.tensor_scalar(out=bk_, in0=CC3[:, :, h], scalar1=rc_, scalar2=-1.0, op0=ALU.subtract, op1=ALU.mult),
                     reads=(b_cc, b_rcol[pb]), writes=(b_bk[pb],))
                P.op(DVE, lambda e: e.tensor_scalar(out=bkp_, in0=bk_[:, 16 + 4 * s:20 + 4 * s], scalar1=PEN, scalar2=0.0, op0=ALU.add, op1=ALU.add),
                     reads=(b_bk[pb], b_pcore), writes=(b_bkp[pb],))
                P.op(DVE, lambda e: e.tensor_copy(HH[0:65, :], A0[0:65, :]), reads=(b_a0,), writes=(b_hh,))
                P.op(DVE, lambda e: e.tensor_tensor(A1[0:65, :], A0[0:65, :], HH[0:65, :], ALU.subtract), reads=(b_a0, b_hh), writes=(b_a1,))
                P.op(DVE, lambda e: e.tensor_copy(MM[0:65, :], A1[0:65, :]), reads=(b_a1,), writes=(b_mm,))
                P.op(DVE, lambda e: e.tensor_tensor(A2[0:65, :], A1[0:65, :], MM[0:65, :], ALU.subtract), reads=(b_a1, b_mm), writes=(b_a2,))
                P.op(DVE, lambda e: e.tensor_copy(cq_[0:1, :], HH[0:1, :]), reads=(b_hh,), writes=(b_cq3[pb],))
                P.op(DVE, lambda e: e.tensor_copy(cq_[32:33, :], MM[32:33, :]), reads=(b_mm,), part=(b_cq3[pb],))
                P.op(DVE, lambda e: e.tensor_copy(cq_[64:65, :], A2[64:65, :]), reads=(b_a2,), part=(b_cq3[pb],))

            slots = [(h, s) for h in range(H_A) for s in range(4)]
            blocks = []
            for (h, s) in slots:
                lst = [("for", j, r) for j in range(s + 1) for r in range(4)] + \
                      [("own", j, r) for j in range(s + 1) for r in range(4)]
                for n, (kind, j, r) in enumerate(lst):
                    blocks.append(dict(h=h, s=s, kind=kind, j=j, r=r, first=(n == 0), last=(n == len(lst) - 1)))
            state = {}

            def fox_qk(bk):
                h, s, kind, j, r = bk["h"], bk["s"], bk["kind"], bk["j"], bk["r"]
                hb, pb = h % 2, (h * 4 + s) % 2
                kblk = (0 if kind == "own" else 16) + 4 * j + r
                diag = (kind == "own" and j == s)
                qlo = r * 128 if diag else 0
                ps, bps = S_R.next()
                fns = [lambda e: e.matmul(ps[:, qlo:512], KTv[hb][:, kblk * 128:(kblk + 1) * 128],
                                          QTv[hb][:, s * 512 + qlo:(s + 1) * 512], start=True, stop=False),
                       lambda e: e.matmul(ps[:, qlo:512], ONESB[0:65, :], CQ3[pb][0:65, qlo:512], start=False, stop=not diag)]
                if diag:
                    fns.append(lambda e: e.matmul(ps[:, qlo:qlo + 128], IDB[:, :], TRIB[:, :], start=False, stop=True))
                P.group(PE, fns, reads=(b_kt[hb], b_qt[hb], b_onesb, b_cq3[pb], b_idb, b_trib), writes=(bps,))
                bk.update(ps=ps, bps=bps, kblk=kblk, diag=diag, qlo=qlo)

            def fox_rest(bk):
                h, s, kind, j, r = bk["h"], bk["s"], bk["kind"], bk["j"], bk["r"]
                hb, pb = h % 2, (h * 4 + s) % 2
                ps, bps, kblk, qlo = bk["ps"], bk["bps"], bk["kblk"], bk["qlo"]
                pt, bpt = B16R.next()
                if kind == "for" and j == s:
                    bias, bbias = BKP[pb][:, r:r + 1], b_bkp[pb]
                else:
                    bias, bbias = BK[pb][:, kblk:kblk + 1], b_bk[pb]
                P.op(ACT, lambda e: e.activation(out=pt[:, qlo:512], in_=ps[:, qlo:512], func=AF.Exp, bias=bias),
                     reads=(bps, bbias), writes=(bpt,))
                if bk["first"]:
                    state["ot"], state["bot"] = OT_R.next()
                    state["rs"], state["brs"] = RS_R.next()
                ot, bot, rs, brs = state["ot"], state["bot"], state["rs"], state["brs"]
                fst, lst_ = bk["first"], bk["last"]
                P.group(PE, [lambda e: e.matmul(ot[:, qlo:512], VHv[hb][:, kblk, :], pt[:, qlo:512], start=fst, stop=lst_),
                             lambda e: e.matmul(rs[:, qlo:512], ONESB[:, :], pt[:, qlo:512], start=fst, stop=lst_)],
                        reads=(b_vh[hb], bpt, b_onesb), writes=(bot, brs) if fst else (), part=() if fst else (bot, brs))
                if lst_:
                    ln_, bln = F32R.next()
                    rc, brc = F32R.next()
                    ob, bob = B16R.next()
                    P.op(ACT, lambda e: e.activation(out=ln_, in_=rs, func=AF.Ln), reads=(brs,), writes=(bln,))
                    P.op(ACT, lambda e: e.activation(out=rc, in_=ln_, func=AF.Exp, scale=-1.0), reads=(bln,), writes=(brc,))
                    P.op(DVE, lambda e: e.tensor_tensor(ob, ot, rc, ALU.mult), reads=(bot, brc), writes=(bob,))
                    store(OA[h * 128:(h + 1) * 128, s * 512:(s + 1) * 512], ob, bob, "OA")

            LOOK = 2
            fox_head_loads(0)
            fox_prep_a(0, 0)
            fox_prep(0, 0)
            nblk_ = len(blocks)
            for n in range(min(LOOK, nblk_)):
                fox_qk(blocks[n])
            since_first = 0
            for n, bk in enumerate(blocks):
                if bk["first"]:
                    since_first = 0
                    si = slots.index((bk["h"], bk["s"]))
                    nxt = slots[si + 1] if si + 1 < len(slots) else None
                    if nxt is not None:
                        if nxt[1] == 0:
                            fox_head_loads(nxt[0])
                        fox_prep_a(*nxt)
                if since_first == 3 and nxt is not None:
                    fox_prep(*nxt)
                since_first += 1
                if n + LOOK < nblk_:
                    fox_qk(blocks[n + LOOK])
                fox_rest(bk)
            retire(b_biga, *(b_qt + b_kt + b_vh))
            retire(b_wt[1], *(b_dg + [b_a0, b_a1, b_a2, b_hh, b_mm, b_ll] + b_cq3 + b_bk + b_bkp + b_rcol))


        if stop >= 5:
            KBs = BIGA[:, 0:10240].rearrange("p (g t) -> p g t", g=4)
            VBs = BIGA[:, 10240:15360].rearrange("p (b c) -> p b c", b=20)
            QBv = [BIGA[:, 20480 + i * 4096: 20480 + (i + 1) * 4096].rearrange("p (h q) -> p h q", h=32) for i in range(2)]
            BIASM = WT[0][:, 0:16384].bitcast(F32).rearrange("p (c h q) -> p c h q", c=2, h=32)
            E2 = WT[1][:, 0:4096]
            b_kbs, b_vbs, b_qbv, b_biasm, b_e2 = fresh(b_biga), fresh(b_biga), [fresh(b_biga), fresh(b_biga)], fresh(b_wt[0]), fresh(b_wt[1])
            b_esk, b_esl = Buf(), Buf()
            P.dma(SP, KBs[0:64, :, :], KB.rearrange("(g d) t -> d g t", d=64), reads=(db("KB"),), writes=(b_kbs,))
            P.dma(SP, VBs, VB.rearrange("(b p) c -> p b c", p=128), reads=(db("VB"),), writes=(b_vbs,))
            for g_ in range(4):
                for c_ in range(2):
                    P.op(DVE, lambda e, g_=g_, c_=c_: e.memset(KBs[64:128, g_, c_ * 1280:(c_ + 1) * 1280], 0.0), part=(b_kbs,))
            for i_ in range(2):
                for c_ in range(2):
                    P.op(DVE, lambda e, i_=i_, c_=c_: e.memset(QBv[i_][64:128, c_ * 16:(c_ + 1) * 16, :], 0.0),
                         writes=(b_qbv[i_],) if c_ == 0 else (), part=() if c_ == 0 else (b_qbv[i_],))
            for c_ in range(2):
                P.op(DVE, lambda e, c_=c_: e.memset(E2[64:128, c_ * 2048:(c_ + 1) * 2048], 0.0), part=(b_e2,))
            P.op(POOL, lambda e: e.memset(ESL2[:, :], 0.0), writes=(b_esl,))
            P.dma(SP, WT[0][:, 0:16384].bitcast(F32), bias_raw, writes=(b_biasm,))
            P.dma(POOL, E2[0:64, :], e2c, writes=(b_e2,))
            for c in range(2):
                for h in range(32):
                    P.op(POOL, lambda e, c=c, h=h: e.tensor_tensor(BIASM[:, c, h, :], BIASM[:, c, h, :],
                                                                   CST[:, 384 + c * 128: 512 + c * 128], ALU.add),
                         reads=(b_cst,), writes=(b_biasm,))
            P.dma(SP, ESK[0:32, 0:1], sinks, writes=(b_esk,))
            P.dma(SP, ESK[32:64, 0:1], sinks, part=(b_esk,))
            P.op(ACT, lambda e: e.activation(out=ESK[:, 1:2], in_=ESK[:, 0:1], func=AF.Exp), reads=(b_esk,), part=(b_esk,))
            P.op(DVE, lambda e: e.tensor_copy(ESKB[:, 0:1], ESK[:, 1:2]), reads=(b_esk,), part=(b_esk,))
            P.op(DVE, lambda e: e.tensor_copy(ESK[:, 2:3], ESKB[:, 0:1]), reads=(b_esk,), part=(b_esk,))
            P.op(DVE, lambda e: e.tensor_sub(ESK[:, 3:4], ESK[:, 1:2], ESK[:, 2:3]), reads=(b_esk,), part=(b_esk,))
            P.op(DVE, lambda e: e.tensor_scalar_mul(ESL2[0:32, :], ONESF[0:32, :], ESK[0:32, 2:3]),
                 reads=(b_esk, b_onesf), part=(b_esl,))
            P.op(DVE, lambda e: e.tensor_scalar_mul(ESL2[32:64, :], ONESF[32:64, :], ESK[32:64, 3:4]),
                 reads=(b_esk, b_onesf), part=(b_esl,))
            QBd = QB.rearrange("(h d) t -> d h t", d=64)
            OBd = OB.rearrange("(h d) t -> d h t", d=64)
            import os
            KQ = 64 if os.environ.get('SWA_OLDQK') else 128
            OLDPV = bool(os.environ.get('SWA_OLDPV'))
            SS_R = Ring([PSB[0][:, :], PSB[1][:, :], PSB[2][:, :], PSB[3][:, :]])
            SOT_R = Ring([PSB[4][:, :], PSB[5][:, :]])
            SRS_R = Ring([PSB[6][:, :], PSB[7][:, :]])
            units = []
            for i in range(16):
                for g in range(4):
                    for hh in range(2):
                        units.append(dict(i=i, g=g, hh=hh))

            def swa_A(u):
                i, g, hh = u["i"], u["g"], u["hh"]
                s, r = i // 4, i % 4
                cur = i
                prev = i - 1 if r > 0 else 16 + s
                qb, bqb = QBv[i % 2], b_qbv[i % 2]
                if g == 0 and hh == 0:
                    P.dma(SP, qb[0:64, :, :], QBd[:, :, i * 128:(i + 1) * 128], reads=(db("QB"),), writes=(bqb,))
                h0 = g * 8 + hh * 4
                pts = []
                for which, kblk in ((1, cur), (0, prev)):
                    ps, bps = SS_R.next()
                    P.group(PE, [lambda e, ps=ps, kblk=kblk: e.matmul(ps, KBs[0:KQ, g, kblk * 128:(kblk + 1) * 128],
                                                                     qb[0:KQ, h0:h0 + 4, :], start=True, stop=True)],
                            reads=(b_kbs, bqb), writes=(bps,))
                    tmp, btmp = F32R.next()
                    pt, bpt = B16R.next()
                    P.op(DVE, lambda e, ps=ps, tmp=tmp, which=which: e.tensor_tensor(
                        tmp.rearrange("p (h q) -> p h q", h=4), ps.rearrange("p (h q) -> p h q", h=4),
                        BIASM[:, which, h0:h0 + 4, :], ALU.add), reads=(bps, b_biasm), writes=(btmp,))
                    if which == 0 and r == 0 and not os.environ.get('SWA_NOPEN'):
                        P.op(ACT, lambda e, pt=pt, tmp=tmp: e.activation(out=pt, in_=tmp, func=AF.Exp, bias=PCORE[:, 3 + s:4 + s]),
                             reads=(btmp, b_pcore), writes=(bpt,))
                    else:
                        P.op(ACT, lambda e, pt=pt, tmp=tmp: e.activation(out=pt, in_=tmp, func=AF.Exp),
                             reads=(btmp,), writes=(bpt,))
                    pts.append((pt, bpt, kblk))
                u.update(pts=pts, h0=h0)

            def swa_B(u):
                i, g, h0 = u["i"], u["g"], u["h0"]
                ot, bot = SOT_R.next()
                rs, brs = SRS_R.next()
                (ptc, bptc, kc_), (ptp, bptp, kp_) = u["pts"]
                vc0 = g * 64 if g < 3 else 128
                u["pr0"] = 0 if g < 3 else 64
                P.group(PE, [lambda e: e.matmul(ot, VBs[:, kc_, vc0:vc0 + 128], ptc, start=True, stop=False),
                             lambda e: e.matmul(ot, VBs[:, kp_, vc0:vc0 + 128], ptp, start=False, stop=True)],
                        reads=(b_vbs, bptc, bptp), writes=(bot,))
                P.group(PE, [lambda e: e.matmul(rs, ONESB[:, :], ptc, start=True, stop=False),
                             lambda e: e.matmul(rs, ONESB[:, :], ptp, start=False, stop=False),
                             lambda e: e.matmul(rs, ESL2[:, :], E2[:, h0 * 128:(h0 + 4) * 128], start=False, stop=True)],
                        reads=(b_onesb, bptc, bptp, b_esl, b_e2), writes=(brs,))
                u.update(ot=ot, bot=bot, rs=rs, brs=brs)

            def swa_C(u):
                i, h0 = u["i"], u["h0"]
                ot, bot, rs, brs = u["ot"], u["bot"], u["rs"], u["brs"]
                ln_, bln = F32R.next()
                rc, brc = F32R.next()
                ob, bob = B16R.next()
                pr = slice(u["pr0"], u["pr0"] + 64)
                P.op(ACT, lambda e: e.activation(out=ln_[pr, :], in_=rs[pr, :], func=AF.Ln), reads=(brs,), writes=(bln,))
                P.op(ACT, lambda e: e.activation(out=rc[pr, :], in_=ln_[pr, :], func=AF.Exp, scale=-1.0), reads=(bln,), writes=(brc,))
                P.op(DVE, lambda e: e.tensor_tensor(ob[pr, :], ot[pr, :], rc[pr, :], ALU.mult), reads=(bot, brc), writes=(bob,))
                store(OBd[:, h0:h0 + 4, i * 128:(i + 1) * 128], ob[pr, :].rearrange("p (h q) -> p h q", h=4), bob, "OB")

            swa_A(units[0])
            for n, u in enumerate(units):
                if n + 1 < len(units):
                    swa_A(units[n + 1])
                swa_B(u)
                swa_C(u)
            retire(b_biga, b_kbs, b_vbs, *b_qbv)
            retire(b_wt[0], b_biasm)
            retire(b_wt[1], b_e2)

        if stop >= 6:
            for tg in range(2):
                OAg = BIGA[:, 0:16384].rearrange("p (kc t) -> p kc t", kc=16)
                OBg = BIGA[:, 16384:32768].rearrange("p (kc t) -> p kc t", kc=16)
                b_oag, b_obg = fresh(b_biga), fresh(b_biga)
                P.dma(SP, OAg, OA.rearrange("(kc p) t -> p kc t", p=128)[:, :, tg * TG:(tg + 1) * TG], reads=(db("OA"),), writes=(b_oag,))
                P.dma(SP, OBg, OB.rearrange("(kc p) t -> p kc t", p=128)[:, :, tg * TG:(tg + 1) * TG], reads=(db("OB"),), writes=(b_obg,))
                for n in range(8):
                    wt, bwt = next_wt(32, 512)
                    wload(wt[:, 0:16, :], bwt, w_a, 0, 16, n * 512, 512)
                    wload(wt[:, 16:32, :], bwt, w_b, 0, 16, n * 512, 512, first=False)
                    def p4_unit(c, tb, n=n, tg=tg, wt=wt, bwt=bwt, OAg=OAg, OBg=OBg, b_oag=b_oag, b_obg=b_obg):
                        if True:
                            r0, t0 = n * 512 + c * 128, tg * TG + tb * 512
                            gat, bgat = B16R.next()
                            gbt, bgbt = B16R.next()
                            P.dma(SP, gat, GA[r0:r0 + 128, t0:t0 + 512], reads=(db("GA"),), writes=(bgat,))
                            P.dma(SP, gbt, GB[r0:r0 + 128, t0:t0 + 512], reads=(db("GB"),), writes=(bgbt,))
                            psa, bpsa = PSR.next()
                            psb, bpsb = PSR.next()
                            P.group(PE, [lambda e, kc=kc, psa=psa: e.matmul(psa, wt[:, kc, c * 128:(c + 1) * 128], OAg[:, kc, tb * 512:(tb + 1) * 512],
                                                                            start=(kc == 0), stop=(kc == 15)) for kc in range(16)] +
                                        [lambda e, kc=kc, psb=psb: e.matmul(psb, wt[:, 16 + kc, c * 128:(c + 1) * 128], OBg[:, kc, tb * 512:(tb + 1) * 512],
                                                                            start=(kc == 0), stop=(kc == 15)) for kc in range(16)],
                                    reads=(bwt, b_oag, b_obg), writes=(bpsa, bpsb))
                            t1, bt1 = F32R.next()
                            t2, bt2 = F32R.next()
                            mx, bmx = B16R.next()
                            P.op(DVE, lambda e, t1=t1, psa=psa, gat=gat: e.tensor_tensor(t1, psa, gat, ALU.mult), reads=(bpsa, bgat), writes=(bt1,))
                            P.op(DVE, lambda e, t2=t2, psb=psb, gbt=gbt: e.tensor_tensor(t2, psb, gbt, ALU.mult), reads=(bpsb, bgbt), writes=(bt2,))
                            P.op(DVE, lambda e, mx=mx, t1=t1, t2=t2: e.tensor_tensor(mx, t1, t2, ALU.add), reads=(bt1, bt2), writes=(bmx,))
                            store(MX[r0:r0 + 128, t0:t0 + 512], mx, bmx, "MX")
                    for c in range(4):
                        for tb in range(2):
                            p4_unit(c, tb)
                retire(b_biga, b_oag, b_obg)

        def resid_gemm(A, bA, kcn, wsrc, k0, res, res_name, dst, dst_name, tg, after_tile=None):
            for n in range(8):
                wt, bwt = next_wt(kcn, 512)
                wload(wt, bwt, wsrc, k0, kcn, n * 512, 512)
                for t8 in range(8):
                    rows = slice(tg * TG + t8 * 128, tg * TG + (t8 + 1) * 128)
                    cols = slice(n * 512, (n + 1) * 512)
                    xr, bxr = F32R.next()
                    P.dma(SP, xr, res[rows, cols], reads=(db(res_name),) if res_name else (), writes=(bxr,))

                    def evac(ps, bps, xr=xr, bxr=bxr, rows=rows, cols=cols):
                        st, bst = F32R.next()
                        P.op(DVE, lambda e: e.tensor_tensor(st, ps, xr, ALU.add), reads=(bps, bxr), writes=(bst,))
                        store(dst[rows, cols], st, bst, dst_name)
                    tm_block(wt, bwt, 0, 512, kcn, A, bA, t8, evac)
                mm_flush()
                if after_tile is not None:
                    after_tile(n)

        if stop >= 7:
            for tg in range(2):
                MXg = BIGA[:, 0:32768].rearrange("p (kc t) -> p kc t", kc=32)
                b_mxg = fresh(b_biga)
                P.dma(SP, MXg, MX.rearrange("(kc p) t -> p kc t", p=128)[:, :, tg * TG:(tg + 1) * TG], reads=(db("MX"),), writes=(b_mxg,))
                resid_gemm(MXg, b_mxg, 32, w_o, 0, x_own, None, X1, "X1", tg)
                retire(b_biga, b_mxg)

        if stop >= 8:
            for tg in range(2):
                hT = build_hT(X1, tg * TG, TG, GT2, b_gt2, (db("X1"),))
                for j in range(D_FF // 256):
                    wt, bwt = next_wt(32, 512)
                    wload(wt[:, :, 0:256], bwt, w_g, 0, 32, j * 256, 256)
                    wload(wt[:, :, 256:512], bwt, w_u, 0, 32, j * 256, 256, first=False)
                    def p6_unit(c, tb, j=j, tg=tg, wt=wt, bwt=bwt, hT=hT):
                        if True:
                            psg, bpsg = PSR.next()
                            psu, bpsu = PSR.next()
                            P.group(PE, [lambda e, kc=kc, psg=psg: e.matmul(psg, wt[:, kc, c * 128:(c + 1) * 128], hT[:, kc, tb * 512:(tb + 1) * 512],
                                                                            start=(kc == 0), stop=(kc == 31)) for kc in range(32)] +
                                        [lambda e, kc=kc, psu=psu: e.matmul(psu, wt[:, kc, 256 + c * 128:256 + (c + 1) * 128], hT[:, kc, tb * 512:(tb + 1) * 512],
                                                                            start=(kc == 0), stop=(kc == 31)) for kc in range(32)],
                                    reads=(bwt, b_biga), writes=(bpsg, bpsu))
                            sg, bsg = F32R.next()
                            hd, bhd = B16R.next()
                            P.op(ACT, lambda e, sg=sg, psg=psg: e.activation(out=sg, in_=psg, func=AF.Silu), reads=(bpsg,), writes=(bsg,))
                            P.op(DVE, lambda e, hd=hd, sg=sg, psu=psu: e.tensor_tensor(hd, psu, sg, ALU.mult), reads=(bpsu, bsg), writes=(bhd,))
                            r0, t0 = j * 256 + c * 128, tg * TG + tb * 512
                            store(HD[r0:r0 + 128, t0:t0 + 512], hd, bhd, "HD")
                    for c in range(2):
                        for tb in range(2):
                            p6_unit(c, tb)

        def final_block_slow(t8):
            yb, byb = XB[:, :], b_xb
            P.dma(ACT, yb, Y2[t8 * 128:(t8 + 1) * 128, :], reads=(db("Y2_0_2"),), writes=(byb,))
            (ss, bss), (rs, brs) = smallr.next(), smallr.next()
            P.op(DVE, lambda e: e.memset(ss, 0.0), writes=(bss,))
            P.op(ACT, lambda e: e.activation(out=XN[0][:, :], in_=yb, func=AF.Square, accum_out=ss), reads=(byb,), writes=(b_xn[0], bss))
            P.op(DVE, lambda e: e.tensor_scalar(out=rs, in0=ss, scalar1=1.0 / D, scalar2=EPS, op0=ALU.mult, op1=ALU.add), reads=(bss,), writes=(brs,))
            P.op(ACT, lambda e: e.activation(out=rs, in_=rs, func=AF.Sqrt), reads=(brs,), writes=(brs,))
            P.op(DVE, lambda e: e.reciprocal(rs, rs), reads=(brs,), writes=(brs,))
            fgh = XN[1][:, :].bitcast(F32)
            for hf in range(2):
                P.dma(ACT, fgh, fgv[:, hf * 2048:(hf + 1) * 2048].partition_broadcast(128).rearrange("p a b -> p (a b)"), writes=(b_xn[1],))
                P.op(DVE, lambda e, hf=hf: e.scalar_tensor_tensor(out=yb[:, hf * 2048:(hf + 1) * 2048], in0=yb[:, hf * 2048:(hf + 1) * 2048],
                                                                  scalar=rs, in1=fgh, op0=ALU.mult, op1=ALU.mult),
                     reads=(brs, b_xn[1]), part=(byb,))
            P.dma(ACT, OUT[t8 * 128:(t8 + 1) * 128, :], yb, reads=(byb,), part=(db("OUT"),))

        if stop >= 9:
            for tg in range(2):
                k0 = 0
                for ks, kcn in enumerate(KSPLIT):
                    HDh = BIGA[:, 0:kcn * TG].rearrange("p (kc t) -> p kc t", kc=kcn)
                    b_hdh = fresh(b_biga)
                    P.dma(SP, HDh, HD[k0:k0 + kcn * 128, tg * TG:(tg + 1) * TG].rearrange("(kc p) t -> p kc t", p=128),
                          reads=(db("HD"),), writes=(b_hdh,))
                    at = (lambda n: final_block_slow(n)) if (tg == 1 and ks == 0 and stop >= 10) else None
                    resid_gemm(HDh, b_hdh, kcn, w_d, k0, X1 if ks == 0 else Y2, "X1" if ks == 0 else f"Y2_{tg}_{ks - 1}", Y2, f"Y2_{tg}_{ks}", tg,
                               after_tile=at)
                    retire(b_biga, b_hdh)
                    k0 += kcn * 128

        if stop >= 10:
            FG = WT[0][:, 0:8192].bitcast(F32)
            b_fg = fresh(b_wt[0])
            P.dma(SP, FG, fgv.partition_broadcast(128).rearrange("p a b -> p (a b)"), writes=(b_fg,))
            ybs = [(XB[:, :], b_xb), (WT[1][:, 0:8192].bitcast(F32), b_wt[1])]
            for t8 in range(8, 16):
                yb, byb = ybs[t8 % 2]
                xn, bxn = XN[t8 % 2][:, :], b_xn[t8 % 2]
                P.dma(SP, yb, Y2[t8 * 128:(t8 + 1) * 128, :], reads=(db("Y2_1_2"),), writes=(byb,))
                ss, bss = smallr.next()
                rs, brs = smallr.next()
                P.op(DVE, lambda e, ss=ss: e.memset(ss, 0.0), writes=(bss,))
                P.op(ACT, lambda e, xn=xn, yb=yb, ss=ss: e.activation(out=xn, in_=yb, func=AF.Square, accum_out=ss),
                     reads=(byb,), writes=(bxn, bss))
                P.op(DVE, lambda e, rs=rs, ss=ss: e.tensor_scalar(out=rs, in0=ss, scalar1=1.0 / D, scalar2=EPS,
                                                                  op0=ALU.mult, op1=ALU.add), reads=(bss,), writes=(brs,))
                P.op(ACT, lambda e, rs=rs: e.activation(out=rs, in_=rs, func=AF.Sqrt), reads=(brs,), writes=(brs,))
                P.op(DVE, lambda e, rs=rs: e.reciprocal(rs, rs), reads=(brs,), writes=(brs,))
                P.op(DVE, lambda e, yb=yb, rs=rs: e.scalar_tensor_tensor(out=yb, in0=yb, scalar=rs, in1=FG, op0=ALU.mult, op1=ALU.mult),
                     reads=(brs, b_fg), writes=(byb,))
                P.dma(SP, OUT[t8 * 128:(t8 + 1) * 128, :], yb, reads=(byb,), part=(db("OUT"),))

        for q in (SP, ACT, POOL):
            for sl in q.slots:
                SP.wait(sl, sl.v)

        global LAST_ENGS
        LAST_ENGS = [PE, ACT, DVE, POOL, SP]
        block.tensor(lambda e: _emit(e, PE.ops))
        block.scalar(lambda e: _emit(e, ACT.ops))
        block.vector(lambda e: _emit(e, DVE.ops))
        block.gpsimd(lambda e: _emit(e, POOL.ops))
        block.sync(lambda e: _emit(e, SP.ops))
    return nc


def _t5_bucket(dist):
    nb, md = 32, 128
    max_exact = nb // 2
    small = dist < max_exact
    large = max_exact + (np.log(np.maximum(dist, 1) / max_exact) / np.log(md / max_exact) * (nb - max_exact)).astype(np.int64)
    large = np.minimum(large, nb - 1)
    return np.where(small, dist, large)


def _consts(rel_bias):
    k = np.arange(128)[:, None]
    q = np.arange(128)[None, :]
    cst = np.zeros((128, 640), np.float32)
    cst[:, 0:128] = np.eye(128, dtype=np.float32)
    cst[:, 128:256] = (k <= q)
    cst[:, 256:384] = np.where(k <= q, 0.0, NEG)
    cst[:, 384:512] = np.where(k > q, 0.0, NEG)
    cst[:, 512:640] = np.where(k <= q, 0.0, NEG)
    bprev = _t5_bucket(np.clip(q + 128 - k, 0, None))
    bcur = _t5_bucket(np.clip(q - k, 0, None))
    braw = np.empty((128, 2, 32, 128), np.float32)
    braw[:, 0] = np.transpose(rel_bias[bprev], (0, 2, 1))
    braw[:, 1] = np.transpose(rel_bias[bcur], (0, 2, 1))
    e2 = np.zeros((64, 32, 128), np.float32)
    for r in range(64):
        e2[r, r % 32, :] = 1.0
    return cst, np.ascontiguousarray(braw.reshape(128, -1)), e2.reshape(64, 4096)


def prep_core(inputs, c, shared):
    x = inputs["x"]
    b, par = c // 2, c % 2
    own = [2 * s + par for s in range(4)]
    forn = [2 * s + (1 - par) for s in range(4)]
    x_own = np.concatenate([x[b, sb * 512:(sb + 1) * 512] for sb in own], axis=0)
    x_for = np.concatenate([x[b, sb * 512:(sb + 1) * 512] for sb in forn], axis=0)
    halo = []
    for sb in own:
        halo.append(x[b, sb * 512 - 128: sb * 512] if sb > 0 else np.zeros((128, D), np.float32))
    pc = np.zeros((128, 8), np.float32)
    pc[:, 0] = float(par)
    pc[:, 1] = 1.0 - float(par)
    pc[:, 2] = NEG if par == 0 else 0.0
    for s, sb in enumerate(own):
        pc[:, 3 + s] = NEG if sb == 0 else 0.0
    m = dict(shared)
    m.update(x_own=np.ascontiguousarray(x_own), x_for=np.ascontiguousarray(x_for),
             x_halo=np.ascontiguousarray(np.concatenate(halo, axis=0)), pcore=pc)
    return m, own


def prep_shared(inputs):
    f = lambda a: np.ascontiguousarray(np.asarray(a, dtype=np.float32))
    cst, braw, e2 = _consts(f(inputs["rel_bias"]))
    return dict(
        n1g=f(np.asarray(inputs["norm1_g"])[0].reshape(KC, 128).T),
        n2g=f(np.asarray(inputs["norm2_g"])[0].reshape(KC, 128).T),
        fgv=f(np.asarray(inputs["final_g"]).reshape(1, D)),
        w_in=f(inputs["w_in"][0]), bfg=f(np.asarray(inputs["b_forget"]).reshape(1, 16)),
        sinks=f(np.asarray(inputs["attn_sinks"]).reshape(32, 1)), bias_raw=braw,
        w_a=f(inputs["w_branch_a"][0]), w_b=f(inputs["w_branch_b"][0]), w_o=f(inputs["w_out"][0]),
        w_g=f(inputs["w_ffn_gate"][0]), w_u=f(inputs["w_ffn_up"][0]), w_d=f(inputs["w_ffn_down"][0]),
        cst=cst, e2c=e2)


def kernel(**inputs):
    inputs = {k: np.asarray(v) for k, v in inputs.items()}
    shared = prep_shared(inputs)
    maps, owns = [], []
    for c in range(8):
        m, own = prep_core(inputs, c, shared)
        maps.append(m)
        owns.append(own)
    nc = build()
    res = run_bass_kernel_spmd(nc, maps, core_ids=list(range(8)))
    out = np.empty((4, SEQ, D), np.float32)
    for c in range(8):
        o = res.results[c]["out"]
        for s, sb in enumerate(owns[c]):
            out[c // 2, sb * 512:(sb + 1) * 512] = o[s * 512:(s + 1) * 512]
    return out
```
